# Optimizing a Trainium2 kernel written in Bass

```python
import jax, jax.numpy as jnp
from jax import lax
import numpy as np

D_MODEL = 1024
BATCH = 2
SEQ = 16384
DEPTH = 2

GRID_W = 64
CTX_LEN = 256
NA_HEADS = 8
HEAD_DIM = 64
NA_WIDTH = NA_HEADS * HEAD_DIM
WIN_H = 8
WIN_W = 16
SG_GROUPS = 4
SG_CHUNK = 128
SG_WIDTH = 512
D_FF = 2816
ROPE_THETA = 10000.0
EPS = 1e-6
N_MOD = 9
Q0, K0, V0 = 0, NA_WIDTH, 2 * NA_WIDTH
U0 = 3 * NA_WIDTH
VS0 = U0 + SG_WIDTH
G0 = VS0 + SG_WIDTH
IN_COLS = G0 + 2 * D_MODEL

kernel_name = "hybrid_na_gmlp_macaron_dit"


def rmsnorm(t, g):
    tf = t.astype(jnp.float32)
    y = tf * lax.rsqrt(jnp.mean(tf * tf, axis=-1, keepdims=True) + EPS)
    return (y * g.astype(jnp.float32)).astype(t.dtype)


def layernorm(t, g, b):
    tf = t.astype(jnp.float32)
    mu = jnp.mean(tf, axis=-1, keepdims=True)
    var = jnp.mean(jnp.square(tf - mu), axis=-1, keepdims=True)
    y = (tf - mu) * lax.rsqrt(var + EPS)
    return (y * g.astype(jnp.float32) + b.astype(jnp.float32)).astype(t.dtype)


def modulate(t, g, shift, scale):
    return rmsnorm(t, g) * (1.0 + scale) + shift


def ada_mod(cond, w, b):
    m = jax.nn.silu(cond) @ w + b
    return jnp.moveaxis(m.reshape(cond.shape[0], N_MOD, D_MODEL), 1, 0)[:, :, None, :]


def swiglu(h, w_up, w_down):
    a, b = jnp.split(h @ w_up, 2, axis=-1)
    return (jax.nn.silu(a) * b) @ w_down


def ffn_sublayer(t, mod, i, g, w_up, w_down):
    h = modulate(t, g, mod[3 * i], mod[3 * i + 1])
    return t + 0.5 * mod[3 * i + 2] * swiglu(h, w_up, w_down)


def heads(t):
    return t.reshape(*t.shape[:-1], NA_HEADS, HEAD_DIM)


def axial_rope(t, rows, cols):
    n_freq = HEAD_DIM // 4
    freqs = ROPE_THETA ** (-jnp.arange(n_freq, dtype=jnp.float32) / n_freq)
    ang = jnp.concatenate([rows[:, None] * freqs, cols[:, None] * freqs], axis=-1)
    cos = jnp.cos(ang)[None, :, None, :]
    sin = jnp.sin(ang)[None, :, None, :]
    tf = t.astype(jnp.float32).reshape(*t.shape[:-1], HEAD_DIM // 2, 2)
    e, o = tf[..., 0], tf[..., 1]
    out = jnp.stack([e * cos - o * sin, e * sin + o * cos], axis=-1).reshape(t.shape)
    return out.astype(t.dtype)


def neighbourhood_attention(q, k, v, k_ctx, v_ctx, rpb):
    B, N, H, hd = q.shape
    rows = N // GRID_W
    win_h = min(WIN_H, rows)
    n_nb = win_h * WIN_W
    scale = hd ** -0.5
    qg = q.reshape(B, rows, GRID_W, H, hd)
    kg = k.reshape(B, rows, GRID_W, H, hd)
    vg = v.reshape(B, rows, GRID_W, H, hd)
    cols = jnp.arange(GRID_W)
    col_start = jnp.clip(cols - WIN_W // 2, 0, GRID_W - WIN_W)
    col_idx = col_start[:, None] + jnp.arange(WIN_W)[None, :]
    dc = col_idx - cols[:, None] + (WIN_W - 1)
    rpb_c = rpb[:, :, dc]

    def row_block(r):
        rs = jnp.clip(r - WIN_H // 2, 0, rows - win_h)
        q_r = lax.dynamic_index_in_dim(qg, r, axis=1, keepdims=False)
        k_rows = lax.dynamic_slice_in_dim(kg, rs, win_h, axis=1)
        v_rows = lax.dynamic_slice_in_dim(vg, rs, win_h, axis=1)
        k_nb = jnp.moveaxis(k_rows[:, :, col_idx], 2, 1).reshape(B, GRID_W, n_nb, H, hd)
        v_nb = jnp.moveaxis(v_rows[:, :, col_idx], 2, 1).reshape(B, GRID_W, n_nb, H, hd)
        dr = rs + jnp.arange(win_h) - r + (WIN_H - 1)
        bias = jnp.moveaxis(jnp.take(rpb_c, dr, axis=1), 2, 1).reshape(H, GRID_W, n_nb)
        s_nb = jnp.einsum('bqhd,bqkhd->bhqk', q_r, k_nb).astype(jnp.float32) * scale \
            + bias.astype(jnp.float32)
        s_ctx = jnp.einsum('bqhd,bkhd->bhqk', q_r, k_ctx).astype(jnp.float32) * scale
        p = jax.nn.softmax(jnp.concatenate([s_nb, s_ctx], axis=-1), axis=-1).astype(v.dtype)
        return (jnp.einsum('bhqk,bqkhd->bqhd', p[..., :n_nb], v_nb)
                + jnp.einsum('bhqk,bkhd->bqhd', p[..., n_nb:], v_ctx))

    out = lax.map(row_block, jnp.arange(rows))
    return jnp.moveaxis(out, 0, 1).reshape(B, N, H * hd)


def context_attention(q, k, v):
    B, L, H, hd = q.shape
    s = jnp.einsum('bqhd,bkhd->bhqk', q, k).astype(jnp.float32) * (hd ** -0.5)
    p = jax.nn.softmax(s, axis=-1).astype(v.dtype)
    return jnp.einsum('bhqk,bkhd->bqhd', p, v).reshape(B, L, H * hd)


def chunk_spatial_gating(u, v, ln_g, ln_b, w_s, b_s):
    B, N, _ = v.shape
    vn = layernorm(v, ln_g, ln_b).reshape(B, N // SG_CHUNK, SG_CHUNK, SG_GROUPS, SG_WIDTH // SG_GROUPS)
    s = jnp.einsum('gpq,bnqgc->bnpgc', w_s, vn) + b_s.T[None, None, :, :, None]
    return u * s.reshape(B, N, SG_WIDTH)


def merge_branches(o_a, o_b, g_logits, b_gate, w_pa, w_pb, w_o):
    g_a, g_b = jnp.split(jax.nn.sigmoid(g_logits + b_gate), 2, axis=-1)
    return (g_a * (o_a @ w_pa) + g_b * (o_b @ w_pb)) @ w_o


def setup_inputs(seed: int = 0) -> dict:
    key = jax.random.key(seed)
    ks = jax.random.split(key, 24)
    f32 = jnp.float32
    nrm = lambda k, shape, s: jax.random.normal(k, shape, f32) * s
    D, L = D_MODEL, DEPTH
    return {
        "x": nrm(ks[0], (BATCH, SEQ, D), 1.0),
        "c": nrm(ks[1], (BATCH, D), 1.0),
        "ctx": nrm(ks[2], (BATCH, CTX_LEN, D), 1.0),
        "c_ctx": nrm(ks[3], (D,), 1.0),
        "w_ada": nrm(ks[4], (L, D, N_MOD * D), 0.5 * D ** -0.5),
        "b_ada": nrm(ks[5], (L, N_MOD * D), 0.02),
        "norm_g": 1.0 + nrm(ks[6], (L, 3, D), 0.02),
        "w_ff1_up": nrm(ks[7], (L, D, 2 * D_FF), D ** -0.5),
        "w_ff1_down": nrm(ks[8], (L, D_FF, D), D_FF ** -0.5),
        "w_in": nrm(ks[9], (L, D, IN_COLS), D ** -0.5),
        "b_gate": nrm(ks[10], (L, 2 * D), 0.02),
        "rpb": nrm(ks[11], (L, NA_HEADS, 2 * WIN_H - 1, 2 * WIN_W - 1), 0.1),
        "ln_v_g": 1.0 + nrm(ks[12], (L, SG_WIDTH), 0.02),
        "ln_v_b": nrm(ks[13], (L, SG_WIDTH), 0.02),
        "w_s": nrm(ks[14], (L, SG_GROUPS, SG_CHUNK, SG_CHUNK), 0.5 * SG_CHUNK ** -0.5),
        "b_s": 1.0 + nrm(ks[15], (L, SG_GROUPS, SG_CHUNK), 0.02),
        "w_pa": nrm(ks[16], (L, NA_WIDTH, D), NA_WIDTH ** -0.5),
        "w_pb": nrm(ks[17], (L, SG_WIDTH, D), SG_WIDTH ** -0.5),
        "w_o": nrm(ks[18], (L, D, D), D ** -0.5),
        "w_ff2_up": nrm(ks[19], (L, D, 2 * D_FF), D ** -0.5),
        "w_ff2_down": nrm(ks[20], (L, D_FF, D), D_FF ** -0.5),
        "final_g": 1.0 + nrm(ks[21], (D,), 0.02),
    }


def reference(x, c, ctx, c_ctx, w_ada, b_ada, norm_g, w_ff1_up, w_ff1_down, w_in, b_gate,
              rpb, ln_v_g, ln_v_b, w_s, b_s, w_pa, w_pb, w_o, w_ff2_up, w_ff2_down, final_g):
    N = x.shape[1]
    t = jnp.arange(N)
    pos_r = (t // GRID_W).astype(jnp.float32)
    pos_c = (t % GRID_W).astype(jnp.float32)
    gelu = jax.nn.gelu

    for l in range(DEPTH):
        last = l == DEPTH - 1
        mx = ada_mod(c, w_ada[l], b_ada[l])
        mc = ada_mod(c_ctx[None, :], w_ada[l], b_ada[l])

        x = ffn_sublayer(x, mx, 0, norm_g[l, 0], w_ff1_up[l], w_ff1_down[l])
        ctx = ffn_sublayer(ctx, mc, 0, norm_g[l, 0], w_ff1_up[l], w_ff1_down[l])

        hx = modulate(x, norm_g[l, 1], mx[3], mx[4])
        hc = modulate(ctx, norm_g[l, 1], mc[3], mc[4])
        px = hx @ w_in[l]
        if last:
            pc = hc @ w_in[l][:, K0:U0]
            kc, vc = heads(pc[..., :NA_WIDTH]), heads(pc[..., NA_WIDTH:])
        else:
            pc = hc @ w_in[l]
            kc, vc = heads(pc[..., K0:V0]), heads(pc[..., V0:U0])

        qx = axial_rope(heads(px[..., Q0:K0]), pos_r, pos_c)
        kx = axial_rope(heads(px[..., K0:V0]), pos_r, pos_c)
        o_a = neighbourhood_attention(qx, kx, heads(px[..., V0:U0]), kc, vc, rpb[l])
        o_b = chunk_spatial_gating(gelu(px[..., U0:VS0]), gelu(px[..., VS0:G0]),
                                   ln_v_g[l], ln_v_b[l], w_s[l], b_s[l])
        x = x + mx[5] * merge_branches(o_a, o_b, px[..., G0:], b_gate[l], w_pa[l], w_pb[l], w_o[l])

        if not last:
            o_ac = context_attention(heads(pc[..., Q0:K0]), kc, vc)
            o_bc = chunk_spatial_gating(gelu(pc[..., U0:VS0]), gelu(pc[..., VS0:G0]),
                                        ln_v_g[l], ln_v_b[l], w_s[l], b_s[l])
            ctx = ctx + mc[5] * merge_branches(o_ac, o_bc, pc[..., G0:], b_gate[l],
                                               w_pa[l], w_pb[l], w_o[l])
            ctx = ffn_sublayer(ctx, mc, 2, norm_g[l, 2], w_ff2_up[l], w_ff2_down[l])

        x = ffn_sublayer(x, mx, 2, norm_g[l, 2], w_ff2_up[l], w_ff2_down[l])

    return rmsnorm(x, final_g)
```

```python
import numpy as np
import concourse.bass as bass
import concourse.mybir as mybir
from concourse.bass_utils import run_bass_kernel_spmd

F32 = mybir.dt.float32
BF = mybir.dt.bfloat16
AF = mybir.ActivationFunctionType
ALU = mybir.AluOpType

D = 1024
DFF = 2816
NH = 8
HD = 64
CTX = 256
GW = 64
NCORE = 8
CPB = 4
NP_OWN = 32
HALO = 4
DEPTH = 2
EPS = 1e-6
NEG = -30000.0
DEBUG_OUT = False

ENGS = ['pe', 'act', 'dve', 'pool', 'sp']
BLK_ATTR = {'pe': 'tensor', 'act': 'scalar', 'dve': 'vector', 'pool': 'gpsimd', 'sp': 'sync'}


def cfg():
    ns = NP_OWN + 2 * HALO
    tx = ns * 128
    return dict(NS=ns, TX=tx, TALL=tx + CTX, NT=ns // 4, NPG=CPB * NP_OWN)


class Prog:
    def __init__(self, nc):
        self.nc = nc
        self.ops = {e: [] for e in ENGS}
        self.cnt = {e: 0 for e in ENGS}
        self.dcnt = {}
        self.lastw = {}
        self.readers = {}
        self.waited = {e: {} for e in ENGS}

    def _deps(self, eng, reads, writes):
        toks = []
        for r in reads:
            if r in self.lastw:
                toks.append(self.lastw[r])
        for w in writes:
            if w in self.lastw:
                toks.append(self.lastw[w])
            toks += self.readers.get(w, [])
        need = {}
        for sem, val in toks:
            if sem == eng and eng == 'pe':
                continue
            if self.waited[eng].get(sem, 0) >= val:
                continue
            need[sem] = max(need.get(sem, 0), val)
        for sem, val in need.items():
            self.waited[eng][sem] = val
        return list(need.items())

    def _commit(self, tok, reads, writes):
        for r in reads:
            self.readers.setdefault(r, []).append(tok)
        for w in writes:
            self.lastw[w] = tok
            self.readers[w] = []

    def op(self, eng, fn, reads=(), writes=()):
        waits = self._deps(eng, reads, writes)
        self.cnt[eng] += 1
        tok = (eng, self.cnt[eng])
        self.ops[eng].append((waits, fn, eng, 1))
        self._commit(tok, reads, writes)

    def dma(self, q, key, fn, reads=(), writes=()):
        waits = self._deps(q, reads, writes)
        self.dcnt[key] = self.dcnt.get(key, 0) + 16
        tok = ('d:' + key, self.dcnt[key])
        self.ops[q].append((waits, fn, tok[0], 16))
        self._commit(tok, reads, writes)

    def barrier(self):
        allt = [(e, self.cnt[e]) for e in ENGS if self.cnt[e] > 0]
        allt += [('d:' + k, v) for k, v in self.dcnt.items()]
        for e in ENGS:
            need = [(sem, v) for sem, v in allt if self.waited[e].get(sem, 0) < v]
            for sem, v in need:
                self.waited[e][sem] = v
            self.ops[e].append((need, None, None, 0))
        self.lastw = {}
        self.readers = {}

    def emit(self, block):
        nc = self.nc
        names = [e for e in ENGS] + ['d:' + k for k in self.dcnt]
        sems = {n: nc.alloc_semaphore('s_' + n.replace(':', '_')) for n in names}
        for e in ENGS:
            ops = self.ops[e]

            def body(eng, ops=ops):
                for waits, fn, semname, inc in ops:
                    for sem, v in waits:
                        eng.wait_ge(sems[sem], v)
                    if fn is not None:
                        ins = fn(eng)
                        ins.then_inc(sems[semname], inc)
            getattr(block, BLK_ATTR[e])(body)


class Arena:
    def __init__(self, nc):
        self.nc = nc
        self.base = ((nc.sbuf_base + 63) // 64) * 64
        self.top = nc.sbuf_top
        self.p = self.base
        self.n = 0

    def alloc(self, shape, dtype):
        esz = 4 if dtype == F32 else 2
        nb = esz
        for s in shape[1:]:
            nb *= s
        nb = ((nb + 63) // 64) * 64
        off = self.p
        self.p += nb
        assert self.p <= self.top, f"SBUF overflow {self.p} > {self.top}"
        self.n += 1
        return self.nc.alloc_sbuf_tensor_at(f"sb{self.n}", list(shape), dtype, offset=off)

    def mark(self):
        return self.p

    def reset(self, m):
        self.p = m


class PsumPool:
    def __init__(self, ps):
        self.ps = ps
        self.order = list(range(8))
        self.i = 0

    def set_order(self, order):
        self.order = list(order)
        self.i = 0

    def bank(self):
        b = self.order[self.i % len(self.order)]
        self.i += 1
        return self.ps[:, b * 512:(b + 1) * 512], f"ps{b}"

    def fixed(self, b, nb=1):
        return self.ps[:, b * 512:(b + nb) * 512], [f"ps{b + k}" for k in range(nb)]


def build_program():
    C = cfg()
    NS, TX, TALL, NT = C['NS'], C['TX'], C['TALL'], C['NT']
    nc = bass.Bass("TRN2", target_bir_lowering=False)

    def din(name, shape, dt=F32):
        return nc.dram_tensor(name, list(shape), dt, kind="ExternalInput").ap()

    def dscr(name, shape, dt):
        kind = "ExternalOutput" if DEBUG_OUT else "Internal"
        return nc.dram_tensor(name, list(shape), dt, kind=kind).ap()

    xT_in = din("xT", [D, TALL])
    cosT = din("cosT", [128, TX])
    sinT = din("sinT", [128, TX])
    condT = din("condT", [128, 16])
    ident_d = din("ident", [128, 128])
    fgT = din("fgT", [128, 8])
    Wl = []
    for l in range(DEPTH):
        Wl.append(dict(
            w_ada=din(f"w_ada{l}", [D, 9 * D]),
            b_adaT=din(f"b_adaT{l}", [128, 72]),
            ngT=din(f"ngT{l}", [128, 24]),
            up1=din(f"up1_{l}", [D, 2 * DFF]), dn1=din(f"dn1_{l}", [DFF, D]),
            up2=din(f"up2_{l}", [D, 2 * DFF]), dn2=din(f"dn2_{l}", [DFF, D]),
            w_in=din(f"w_in{l}", [D, 4608]),
            w_sw=din(f"w_sw{l}", [D, 1024]),
            bgT=din(f"bgT{l}", [128, 16]),
            bt=din(f"bt{l}", [128, 8 * 5 * 128 + 4 * 8 * 6 * 128]),
            lng=din(f"lng{l}", [128, 512]), lnb=din(f"lnb{l}", [128, 512]),
            wsT=din(f"wsT{l}", [128, 512]),
            bs=din(f"bs{l}", [1, 512]),
            w_pa=din(f"w_pa{l}", [512, D]), w_pb=din(f"w_pb{l}", [512, D]),
            w_o=din(f"w_o{l}", [D, D]),
        ))
    yT = nc.dram_tensor("yT", [D, NP_OWN * 128], F32, kind="ExternalOutput").ap()
    XS = [dscr(f"xs{k}", [D, TALL], F32) for k in range(3)]
    HX = dscr("hxs", [D, TALL], BF)
    KS = dscr("kss", [512, TALL], BF)
    VS = dscr("vss", [NS + 2, 128, 520], BF)
    OA = dscr("oas", [512, TALL], BF)

    ps_t = nc.alloc_psum_tensor("psall", [128, 4096], F32)
    ps = ps_t[:]
    PP = PsumPool(ps)
    A = Arena(nc)
    P = Prog(nc)

    ones_bf = A.alloc([128, 128], BF)
    ident_bf = A.alloc([128, 128], BF)
    eps_t = A.alloc([128, 1], F32)
    onesz = A.alloc([64, 128], BF)
    cond_f = A.alloc([128, 16], F32)
    cond_bf = A.alloc([128, 16], BF)
    fg_t = A.alloc([128, 8], F32)
    mv = A.alloc([128, 144], F32)
    GS = A.alloc([128, 48], F32)
    HG = A.alloc([128, 32], F32)
    b_ada_t = A.alloc([128, 72], F32)
    ng_t = A.alloc([128, 24], F32)

    P.op('dve', lambda e: e.memset(ones_bf[:], 1.0), writes=['ones'])
    P.op('dve', lambda e: e.memset(eps_t[:], EPS), writes=['eps'])
    P.op('dve', lambda e: e.memset(onesz[:], 0.0), writes=['onesz'])
    P.op('dve', lambda e: e.memset(onesz[0:1, :], 1.0), writes=['onesz'])
    P.op('dve', lambda e: e.memset(onesz[32:33, :], 1.0), writes=['onesz'])
    P.dma('pool', 'c0', lambda e: e.dma_start(out=ident_bf[:], in_=ident_d), writes=['ident'])
    P.dma('sp', 'c1', lambda e: e.dma_start(out=cond_f[:], in_=condT), writes=['condf'])
    P.dma('sp', 'c3', lambda e: e.dma_start(out=fg_t[:], in_=fgT), writes=['fg'])
    P.op('act', lambda e: e.activation(out=cond_bf[:], in_=cond_f[:], func=AF.Silu),
         reads=['condf'], writes=['condb'])
    P.barrier()
    base_mark = A.mark()

    def mvcol(idx, c, s):
        k = ((idx * 8 + c) * 2 + s)
        return mv[:, k:k + 1]

    def gscol(i, c, s):
        k = ((i * 8 + c) * 2 + s)
        return GS[:, k:k + 1]

    def hgcol(kk, c, s):
        k = ((kk * 8 + c) * 2 + s)
        return HG[:, k:k + 1]

    def xtiles(lo, hi):
        return [dict(t0=t * 512, N=512, s=0, idx=t) for t in range(lo, hi)]
    CT = dict(t0=TX, N=CTX, s=1, idx=NT)

    def wload(dst, src, key, res, nsplit, rows_per, cols):
        for k in range(nsplit):
            P.dma('pool', key,
                  lambda e, k=k: e.dma_start(out=dst[:, k * cols:(k + 1) * cols],
                                             in_=src[k * 128:(k + 1) * 128, :]),
                  writes=[res])

    def layer_setup(l):
        W = Wl[l]
        A.reset(base_mark)
        wp = [A.alloc([128, 8 * 1024], BF) for _ in range(2)]
        P.dma('sp', 'c1', lambda e: e.dma_start(out=b_ada_t[:], in_=W['b_adaT']), writes=['bada'])
        P.dma('sp', 'c3', lambda e: e.dma_start(out=ng_t[:], in_=W['ngT']), writes=['ng'])
        psm, psr = PP.fixed(0)
        for i in range(9):
            b = i % 2
            for kc in range(8):
                P.dma('pool', f'wp{b}',
                      lambda e, kc=kc, i=i, b=b: e.dma_start(
                          out=wp[b][:, kc * 1024:(kc + 1) * 1024],
                          in_=W['w_ada'][kc * 128:(kc + 1) * 128, i * 1024:(i + 1) * 1024]),
                      writes=[f'wp{b}'])

            def mm(e, i=i, b=b):
                ins = None
                for c in range(8):
                    for kc in range(8):
                        col = (i * 8 + c) * 2
                        ins = e.matmul(psm[:, col:col + 2],
                                       wp[b][:, kc * 1024 + c * 128: kc * 1024 + (c + 1) * 128],
                                       cond_bf[:, kc * 2:kc * 2 + 2],
                                       start=(kc == 0), stop=(kc == 7))
                return ins
            P.op('pe', mm, reads=[f'wp{b}', 'condb'], writes=psr)
        for s in range(2):
            P.op('dve', lambda e, s=s: e.tensor_tensor(
                out=mv[:, s:144:2], in0=psm[:, s:144:2], in1=b_ada_t[:], op=ALU.add),
                reads=psr + ['bada'], writes=['mv'])
        for i in range(3):
            for s in range(2):
                a0 = ((3 * i + 1) * 8) * 2 + s
                g0 = (i * 8) * 2 + s
                P.op('dve', lambda e, a0=a0, g0=g0, i=i: e.scalar_tensor_tensor(
                    out=GS[:, g0:g0 + 15:2], in0=mv[:, a0:a0 + 15:2], scalar=1.0,
                    in1=ng_t[:, i * 8:(i + 1) * 8], op0=ALU.add, op1=ALU.mult),
                    reads=['mv', 'ng'], writes=['GS'])
        for kk, i in enumerate((0, 2)):
            for s in range(2):
                a0 = ((3 * i + 2) * 8) * 2 + s
                h0 = (kk * 8) * 2 + s
                P.op('dve', lambda e, a0=a0, h0=h0: e.tensor_scalar(
                    out=HG[:, h0:h0 + 15:2], in0=mv[:, a0:a0 + 15:2], scalar1=0.5, scalar2=None,
                    op0=ALU.mult), reads=['mv'], writes=['HG'])
        P.barrier()

    def norm_ops(xt, rx, hx, rh, N, i, s, rstd, tmp):
        rhs_names = [f'{rh}{c}' for c in range(8)]
        P.op('act', lambda e: e.activation(out=hx[:, 0:8 * N], in_=xt[:, 0:8 * N], func=AF.Square),
             reads=[rx], writes=rhs_names)
        bk, br = PP.bank()

        def mm(e):
            ins = None
            for c in range(8):
                ins = e.matmul(bk[:, 0:N], ones_bf[:], hx[:, c * N:(c + 1) * N],
                               start=(c == 0), stop=(c == 7))
            return ins
        P.op('pe', mm, reads=rhs_names + ['ones'], writes=[br])
        P.op('act', lambda e: e.activation(out=rstd[:, 0:N], in_=bk[:, 0:N], func=AF.Sqrt,
                                           bias=eps_t[:], scale=1.0 / D),
             reads=[br, 'eps'], writes=['rstd'])
        P.op('dve', lambda e: e.reciprocal(out=rstd[:, 0:N], in_=rstd[:, 0:N]),
             reads=['rstd'], writes=['rstd'])
        for c in range(8):
            tb = tmp[c % 2]
            P.op('dve', lambda e, c=c, tb=tb: e.scalar_tensor_tensor(
                out=tb[:, 0:N], in0=xt[:, c * N:(c + 1) * N], scalar=gscol(i, c, s),
                in1=rstd[:, 0:N], op0=ALU.mult, op1=ALU.mult),
                reads=[rx, 'rstd', 'GS'], writes=[f'tmp{c % 2}'])
            P.op('act', lambda e, c=c, tb=tb: e.activation(
                out=hx[:, c * N:(c + 1) * N], in_=tb[:, 0:N], func=AF.Identity,
                bias=mvcol(3 * i, c, s), scale=1.0),
                reads=[f'tmp{c % 2}', 'mv'], writes=[f'{rh}{c}'])
        return rhs_names

    def xview(dram, t0, N, nchunk=8):
        return dram[:, t0:t0 + N].rearrange("(c p) n -> p c n", p=128)

    def sview(t, N, nchunk=8):
        return t[:, 0:nchunk * N].rearrange("p (c n) -> p c n", c=nchunk)

    def ffn_sweep(l, i, tiles, Xsrc, Xdst, final=False):
        W = Wl[l]
        kk = 0 if i == 0 else 1
        A.reset(base_mark)
        PP.set_order(range(8))
        Wup = A.alloc([128, 8 * 2 * DFF], BF)
        Wdn = A.alloc([128, 22 * D], BF)
        xt = [A.alloc([128, 8 * 512], F32) for _ in range(2)]
        hx = A.alloc([128, 8 * 512], BF)
        act = A.alloc([128, 22 * 512], BF)
        rstd = A.alloc([128, 512], F32)
        tmp = [A.alloc([128, 512], F32) for _ in range(2)]
        sa = [A.alloc([128, 512], F32) for _ in range(2)]
        wload(Wup, W['up1' if i == 0 else 'up2'], 'wup', 'Wup', 8, 128, 2 * DFF)
        wload(Wdn, W['dn1' if i == 0 else 'dn2'], 'wdn', 'Wdn', 22, 128, D)

        def load(ti, b):
            T = tiles[ti]
            P.dma('sp', f'x{b}', lambda e: e.dma_start(out=sview(xt[b], T['N']),
                                                       in_=xview(Xsrc, T['t0'], T['N'])),
                  reads=[f"{id(Xsrc)}:{T['idx']}"], writes=[f'xt{b}'])
        load(0, 0)

        def do_tile(ti, T):
            b = ti % 2
            N, s = T['N'], T['s']
            if ti + 1 < len(tiles):
                load(ti + 1, 1 - b)
            x = xt[b]
            rx = f'xt{b}'
            hn = norm_ops(x, rx, hx, 'hx', N, i, s, rstd, tmp)
            for j in range(22):
                bA, rA = PP.bank()
                bB, rB = PP.bank()

                def mmu(e, j=j, bank=bA, off=0):
                    ins = None
                    for kc in range(8):
                        c0 = kc * 2 * DFF + off + j * 128
                        ins = e.matmul(bank[:, 0:N], Wup[:, c0:c0 + 128], hx[:, kc * N:(kc + 1) * N],
                                       start=(kc == 0), stop=(kc == 7))
                    return ins
                P.op('pe', mmu, reads=hn + ['Wup'], writes=[rA])
                P.op('pe', lambda e, j=j, bank=bB: mmu(e, j, bank, DFF), reads=hn + ['Wup'], writes=[rB])
                sb = sa[j % 2]
                P.op('act', lambda e, bA=bA, sb=sb: e.activation(out=sb[:, 0:N], in_=bA[:, 0:N], func=AF.Silu),
                     reads=[rA], writes=[f'sa{j % 2}'])
                P.op('dve', lambda e, bB=bB, sb=sb, j=j: e.tensor_tensor(
                    out=act[:, j * N:(j + 1) * N], in0=bB[:, 0:N], in1=sb[:, 0:N], op=ALU.mult),
                    reads=[rB, f'sa{j % 2}'], writes=[f'act{j}'])
            an = [f'act{j}' for j in range(22)]
            for oc in range(8):
                bY, rY = PP.bank()

                def mmd(e, oc=oc, bY=bY):
                    ins = None
                    for j in range(22):
                        ins = e.matmul(bY[:, 0:N], Wdn[:, j * D + oc * 128: j * D + (oc + 1) * 128],
                                       act[:, j * N:(j + 1) * N], start=(j == 0), stop=(j == 21))
                    return ins
                P.op('pe', mmd, reads=an + ['Wdn'], writes=[rY])
                P.op('dve', lambda e, oc=oc, bY=bY: e.scalar_tensor_tensor(
                    out=x[:, oc * N:(oc + 1) * N], in0=bY[:, 0:N], scalar=hgcol(kk, oc, s),
                    in1=x[:, oc * N:(oc + 1) * N], op0=ALU.mult, op1=ALU.add),
                    reads=[rY, rx, 'HG'], writes=[rx])
            if final:
                hn2 = [f'hx{c}' for c in range(8)]
                P.op('act', lambda e: e.activation(out=hx[:, 0:8 * N], in_=x[:, 0:8 * N], func=AF.Square),
                     reads=[rx], writes=hn2)
                bk, br = PP.bank()

                def mm2(e, bk=bk):
                    ins = None
                    for c in range(8):
                        ins = e.matmul(bk[:, 0:N], ones_bf[:], hx[:, c * N:(c + 1) * N],
                                       start=(c == 0), stop=(c == 7))
                    return ins
                P.op('pe', mm2, reads=hn2 + ['ones'], writes=[br])
                P.op('act', lambda e, bk=bk: e.activation(out=rstd[:, 0:N], in_=bk[:, 0:N], func=AF.Sqrt,
                                                         bias=eps_t[:], scale=1.0 / D),
                     reads=[br, 'eps'], writes=['rstd'])
                P.op('dve', lambda e: e.reciprocal(out=rstd[:, 0:N], in_=rstd[:, 0:N]),
                     reads=['rstd'], writes=['rstd'])
                for c in range(8):
                    P.op('dve', lambda e, c=c: e.scalar_tensor_tensor(
                        out=x[:, c * N:(c + 1) * N], in0=x[:, c * N:(c + 1) * N], scalar=fg_t[:, c:c + 1],
                        in1=rstd[:, 0:N], op0=ALU.mult, op1=ALU.mult),
                        reads=[rx, 'rstd', 'fg'], writes=[rx])
                o0 = T['t0'] - HALO * 128
                P.dma('sp', f'xs{b}', lambda e, o0=o0: e.dma_start(
                    out=yT[:, o0:o0 + N].rearrange("(c p) n -> p c n", p=128), in_=sview(x, N)),
                    reads=[rx], writes=[f"y:{T['idx']}"])
            else:
                P.dma('sp', f'xs{b}', lambda e: e.dma_start(out=xview(Xdst, T['t0'], N), in_=sview(x, N)),
                      reads=[rx], writes=[f"{id(Xdst)}:{T['idx']}"])
        for ti, T in enumerate(tiles):
            do_tile(ti, T)
        P.barrier()

    def kv_sweep(l, tiles, Xsrc):
        W = Wl[l]
        A.reset(base_mark)
        PP.set_order(range(8))
        Wk = A.alloc([128, 8 * 512], BF)
        Wks = A.alloc([128, 8 * 512], BF)
        Wv = A.alloc([128, 8 * 512], BF)
        xt = [A.alloc([128, 8 * 512], F32) for _ in range(2)]
        hx = A.alloc([128, 8 * 512], BF)
        cs_t = A.alloc([128, 512], F32)
        sn_t = A.alloc([128, 512], F32)
        kt = A.alloc([128, 4 * 512], BF)
        rstd = A.alloc([128, 512], F32)
        tmp = [A.alloc([128, 512], F32) for _ in range(2)]
        t1 = A.alloc([128, 512], F32)
        t2 = A.alloc([128, 512], F32)
        va = A.alloc([128, 4 * 520], BF)
        for kc in range(8):
            P.dma('pool', 'wk', lambda e, kc=kc: e.dma_start(
                out=Wk[:, kc * 512:(kc + 1) * 512], in_=W['w_in'][kc * 128:(kc + 1) * 128, 512:1024]), writes=['Wk'])
            P.dma('pool', 'wk', lambda e, kc=kc: e.dma_start(
                out=Wks[:, kc * 512:(kc + 1) * 512], in_=W['w_sw'][kc * 128:(kc + 1) * 128, 512:1024]), writes=['Wk'])
            P.dma('pool', 'wk', lambda e, kc=kc: e.dma_start(
                out=Wv[:, kc * 512:(kc + 1) * 512], in_=W['w_in'][kc * 128:(kc + 1) * 128, 1024:1536]), writes=['Wk'])
        P.op('dve', lambda e: e.memset(va[:], 1.0), writes=['va'])

        def load(ti, b):
            T = tiles[ti]
            P.dma('sp', f'x{b}', lambda e: e.dma_start(out=sview(xt[b], T['N']),
                                                       in_=xview(Xsrc, T['t0'], T['N'])),
                  writes=[f'xt{b}'])
        load(0, 0)

        def do_tile(ti, T):
            b = ti % 2
            N, s, t0 = T['N'], T['s'], T['t0']
            if ti + 1 < len(tiles):
                load(ti + 1, 1 - b)
            if s == 0:
                P.dma('sp', 'cs', lambda e, t0=t0: e.dma_start(out=cs_t[:], in_=cosT[:, t0:t0 + 512]), writes=['cs'])
                P.dma('sp', 'cs', lambda e, t0=t0: e.dma_start(out=sn_t[:], in_=sinT[:, t0:t0 + 512]), writes=['cs'])
            x = xt[b]
            hn = norm_ops(x, f'xt{b}', hx, 'hx', N, 1, s, rstd, tmp)
            P.dma('sp', 'hxs', lambda e, t0=t0, N=N: e.dma_start(out=xview(HX, t0, N), in_=sview(hx, N)),
                  reads=hn, writes=[f'HX:{T["idx"]}'])
            for kc in range(4):
                bA, rA = PP.bank()

                def mk(e, kc=kc, bank=bA, Wm=Wk):
                    ins = None
                    for k in range(8):
                        ins = e.matmul(bank[:, 0:N], Wm[:, k * 512 + kc * 128: k * 512 + (kc + 1) * 128],
                                       hx[:, k * N:(k + 1) * N], start=(k == 0), stop=(k == 7))
                    return ins
                P.op('pe', mk, reads=hn + ['Wk'], writes=[rA])
                if s == 0:
                    bB, rB = PP.bank()
                    P.op('pe', lambda e, kc=kc, bank=bB: mk(e, kc, bank, Wks), reads=hn + ['Wk'], writes=[rB])
                    P.op('dve', lambda e, bA=bA: e.tensor_tensor(out=t1[:, 0:N], in0=bA[:, 0:N], in1=cs_t[:, 0:N], op=ALU.mult),
                         reads=[rA, 'cs'], writes=['t1'])
                    P.op('dve', lambda e, bB=bB: e.tensor_tensor(out=t2[:, 0:N], in0=bB[:, 0:N], in1=sn_t[:, 0:N], op=ALU.mult),
                         reads=[rB, 'cs'], writes=['t2'])
                    P.op('pool', lambda e, kc=kc: e.tensor_tensor(out=kt[:, kc * N:(kc + 1) * N], in0=t1[:, 0:N], in1=t2[:, 0:N], op=ALU.add),
                         reads=['t1', 't2'], writes=['kt'])
                else:
                    P.op('act', lambda e, kc=kc, bA=bA: e.activation(out=kt[:, kc * N:(kc + 1) * N], in_=bA[:, 0:N], func=AF.Copy),
                         reads=[rA], writes=['kt'])
            P.dma('sp', 'kts', lambda e, t0=t0, N=N: e.dma_start(
                out=KS[:, t0:t0 + N].rearrange("(c p) n -> p c n", p=128), in_=sview(kt, N, 4)),
                reads=['kt'], writes=[f'KS:{T["idx"]}'])
            nsub = N // 128
            for sub in range(nsub):
                bV, rV = PP.bank()

                def mv_(e, sub=sub, bV=bV):
                    ins = None
                    for k in range(8):
                        ins = e.matmul(bV[:, 0:512], hx[:, k * N + sub * 128: k * N + (sub + 1) * 128],
                                       Wv[:, k * 512:(k + 1) * 512], start=(k == 0), stop=(k == 7))
                    return ins
                P.op('pe', mv_, reads=hn + ['Wk'], writes=[rV])
                P.op('act', lambda e, sub=sub, bV=bV: e.activation(
                    out=va[:, sub * 520:(sub + 1) * 520].rearrange("p (h e) -> p h e", e=65)[:, :, 0:64],
                    in_=bV[:, 0:512].rearrange("p (h e) -> p h e", e=64), func=AF.Copy),
                    reads=[rV], writes=['va'])
            sl0 = t0 // 128
            P.dma('sp', 'vas', lambda e, sl0=sl0, nsub=nsub: e.dma_start(
                out=VS[sl0:sl0 + nsub].rearrange("s p e -> p s e"),
                in_=va[:, 0:nsub * 520].rearrange("p (s e) -> p s e", e=520)),
                reads=['va'], writes=[f'VS:{T["idx"]}'])
        for ti, T in enumerate(tiles):
            do_tile(ti, T)
        P.barrier()

    def att_sweep(l, tiles):
        W = Wl[l]
        A.reset(base_mark)
        Wq = A.alloc([128, 8 * 512], BF)
        Wqs = A.alloc([128, 8 * 512], BF)
        Eg = A.alloc([128, 8 * 5 * 128], BF)
        Es = [A.alloc([128, 8 * 6 * 128], BF) for _ in range(4)]
        stg = A.alloc([128, 8 * 6 * 128], F32)
        Kc = A.alloc([128, 4 * 256], BF)
        Vc = A.alloc([128, 2 * 520], BF)
        hx = A.alloc([128, 8 * 512], BF)
        cs_t = A.alloc([128, 512], F32)
        sn_t = A.alloc([128, 512], F32)
        Q = A.alloc([128, 4 * 512], BF)
        t1 = A.alloc([128, 512], F32)
        t2 = A.alloc([128, 512], F32)
        Kb = [A.alloc([128, 4 * 768], BF) for _ in range(2)]
        Vb = [A.alloc([128, 6 * 520], BF) for _ in range(2)]
        Pb = [A.alloc([128, 1024], BF) for _ in range(2)]
        otok = A.alloc([128, 512], BF)
        rden = A.alloc([128, 8], F32)
        oaT = A.alloc([128, 4 * 512], BF)
        for kc in range(8):
            P.dma('pool', 'wk', lambda e, kc=kc: e.dma_start(
                out=Wq[:, kc * 512:(kc + 1) * 512], in_=W['w_in'][kc * 128:(kc + 1) * 128, 0:512]), writes=['Wq'])
            P.dma('pool', 'wk', lambda e, kc=kc: e.dma_start(
                out=Wqs[:, kc * 512:(kc + 1) * 512], in_=W['w_sw'][kc * 128:(kc + 1) * 128, 0:512]), writes=['Wq'])
        P.dma('sp', 'stg', lambda e: e.dma_start(out=stg[:, 0:5120], in_=W['bt'][:, 0:5120]), writes=['stg'])
        P.op('act', lambda e: e.activation(out=Eg[:], in_=stg[:, 0:5120], func=AF.Exp), reads=['stg'], writes=['E'])
        for j in range(4):
            P.dma('sp', 'stg', lambda e, j=j: e.dma_start(out=stg[:], in_=W['bt'][:, 5120 + j * 6144: 5120 + (j + 1) * 6144]),
                  writes=['stg'])
            P.op('act', lambda e, j=j: e.activation(out=Es[j][:], in_=stg[:], func=AF.Exp), reads=['stg'], writes=['E'])
        P.dma('sp', 'kc', lambda e: e.dma_start(out=sview(Kc, 256, 4),
                                                in_=KS[:, TX:TX + 256].rearrange("(c p) n -> p c n", p=128)), writes=['Kc'])
        P.dma('sp', 'kc', lambda e: e.dma_start(out=Vc[:].rearrange("p (s e) -> p s e", e=520),
                                                in_=VS[NS:NS + 2].rearrange("s p e -> p s e")), writes=['Kc'])
        bQA, rQA = PP.fixed(6)
        bQB, rQB = PP.fixed(7)
        Sb = [PP.fixed(0, 2), PP.fixed(2, 2)]
        Ob, rO = PP.fixed(4, 2)
        blkc = [0]

        def do_tile(T):
            N, s, t0 = T['N'], T['s'], T['t0']
            P.dma('sp', 'hxl', lambda e, t0=t0, N=N: e.dma_start(out=sview(hx, N), in_=xview(HX, t0, N)), writes=['hx'])
            if s == 0:
                P.dma('sp', 'cs', lambda e, t0=t0: e.dma_start(out=cs_t[:], in_=cosT[:, t0:t0 + 512]), writes=['cs'])
                P.dma('sp', 'cs', lambda e, t0=t0: e.dma_start(out=sn_t[:], in_=sinT[:, t0:t0 + 512]), writes=['cs'])
            for kc in range(4):
                def mq(e, kc=kc, bank=bQA, Wm=Wq):
                    ins = None
                    for k in range(8):
                        ins = e.matmul(bank[:, 0:N], Wm[:, k * 512 + kc * 128: k * 512 + (kc + 1) * 128],
                                       hx[:, k * N:(k + 1) * N], start=(k == 0), stop=(k == 7))
                    return ins
                P.op('pe', mq, reads=['hx', 'Wq'], writes=rQA)
                if s == 0:
                    P.op('pe', lambda e, kc=kc: mq(e, kc, bQB, Wqs), reads=['hx', 'Wq'], writes=rQB)
                    P.op('dve', lambda e: e.tensor_tensor(out=t1[:, 0:N], in0=bQA[:, 0:N], in1=cs_t[:, 0:N], op=ALU.mult),
                         reads=rQA + ['cs'], writes=['t1'])
                    P.op('dve', lambda e: e.tensor_tensor(out=t2[:, 0:N], in0=bQB[:, 0:N], in1=sn_t[:, 0:N], op=ALU.mult),
                         reads=rQB + ['cs'], writes=['t2'])
                    P.op('pool', lambda e, kc=kc: e.tensor_tensor(out=Q[:, kc * N:(kc + 1) * N], in0=t1[:, 0:N], in1=t2[:, 0:N], op=ALU.add),
                         reads=['t1', 't2'], writes=['Q'])
                else:
                    P.op('act', lambda e, kc=kc: e.activation(out=Q[:, kc * N:(kc + 1) * N], in_=bQA[:, 0:N], func=AF.Copy),
                         reads=rQA, writes=['Q'])
            for qb in range(N // 128):
                bb = blkc[0] % 2
                blkc[0] += 1
                if s == 0:
                    slot = t0 // 128 + qb
                    p = slot - HALO
                    lo_o, hi_o = -2, 2
                    etab, NO, eo0 = Eg, 5, 0
                    if p == 0:
                        hi_o = 3
                        etab, NO = Es[0], 6
                    elif p == 1:
                        etab, NO = Es[1], 6
                    elif p == NP_OWN - 2:
                        etab, NO = Es[2], 6
                    elif p == NP_OWN - 1:
                        lo_o = -3
                        etab, NO = Es[3], 6
                    lo = max(slot + lo_o, 0)
                    hi = min(slot + hi_o, NS - 1)
                    eo0 = lo - (slot + lo_o)
                    nk = hi - lo + 1
                    P.dma('sp', f'kb{bb}', lambda e, lo=lo, nk=nk, bb=bb: e.dma_start(
                        out=Kb[bb][:].rearrange("p (c n) -> p c n", c=4)[:, :, 0:nk * 128],
                        in_=KS[:, lo * 128:(lo + nk) * 128].rearrange("(c p) n -> p c n", p=128)),
                        writes=[f'Kb{bb}'])
                    P.dma('sp', f'kb{bb}', lambda e, lo=lo, nk=nk, bb=bb: e.dma_start(
                        out=Vb[bb][:, 0:nk * 520].rearrange("p (s e) -> p s e", e=520),
                        in_=VS[lo:lo + nk].rearrange("s p e -> p s e")),
                        writes=[f'Kb{bb}'])
                else:
                    nk = 0
                ntot = nk + 2
                for h in range(NH):
                    hc, pb = h // 2, 64 * (h % 2)
                    Sps, rS = Sb[h % 2]
                    Pbuf = Pb[h % 2]

                    def qk(e, hc=hc, pb=pb, Sps=Sps, nk=nk, bb=bb, qb=qb):
                        ins = None
                        qv = Q[pb:pb + 64, hc * N + qb * 128: hc * N + (qb + 1) * 128]
                        for kt_ in range(nk):
                            ins = e.matmul(Sps[:, kt_ * 128:(kt_ + 1) * 128],
                                           Kb[bb][pb:pb + 64, hc * 768 + kt_ * 128: hc * 768 + (kt_ + 1) * 128],
                                           qv, start=True, stop=True)
                        for ct in range(2):
                            ins = e.matmul(Sps[:, (nk + ct) * 128:(nk + ct + 1) * 128],
                                           Kc[pb:pb + 64, hc * 256 + ct * 128: hc * 256 + (ct + 1) * 128],
                                           qv, start=True, stop=True)
                        return ins
                    P.op('pe', qk, reads=['Q', f'Kb{bb}', 'Kc'], writes=rS)
                    P.op('act', lambda e, Sps=Sps, Pbuf=Pbuf, ntot=ntot: e.activation(
                        out=Pbuf[:, 0:ntot * 128], in_=Sps[:, 0:ntot * 128], func=AF.Exp, scale=HD ** -0.5),
                        reads=rS, writes=[f'P{h % 2}'])
                    if nk > 0:
                        e0 = (h * NO + eo0) * 128
                        P.op('pool', lambda e, Pbuf=Pbuf, etab=etab, e0=e0, nk=nk: e.tensor_tensor(
                            out=Pbuf[:, 0:nk * 128], in0=Pbuf[:, 0:nk * 128], in1=etab[:, e0:e0 + nk * 128], op=ALU.mult),
                            reads=[f'P{h % 2}', 'E'], writes=[f'P{h % 2}'])

                    def pv(e, h=h, Pbuf=Pbuf, nk=nk, bb=bb):
                        ins = None
                        for kt_ in range(nk):
                            ins = e.matmul(Ob[:, h * 128:h * 128 + 65], Pbuf[:, kt_ * 128:(kt_ + 1) * 128],
                                           Vb[bb][:, kt_ * 520 + h * 65: kt_ * 520 + h * 65 + 65],
                                           start=(kt_ == 0), stop=False)
                        for ct in range(2):
                            ins = e.matmul(Ob[:, h * 128:h * 128 + 65], Pbuf[:, (nk + ct) * 128:(nk + ct + 1) * 128],
                                           Vc[:, ct * 520 + h * 65: ct * 520 + h * 65 + 65],
                                           start=(nk == 0 and ct == 0), stop=(ct == 1))
                        return ins
                    P.op('pe', pv, reads=[f'P{h % 2}', f'Kb{bb}', 'Kc'], writes=[f'O{h}'] + (rO if h == 0 else []))
                on = [f'O{h}' for h in range(NH)]
                P.op('dve', lambda e: e.reciprocal(
                    out=rden[:], in_=Ob[:].rearrange("p (h e) -> p h e", e=128)[:, :, 64]),
                    reads=on, writes=['rden'])
                for h in range(NH):
                    P.op('dve', lambda e, h=h: e.tensor_scalar(
                        out=otok[:, h * 64:(h + 1) * 64], in0=Ob[:, h * 128:h * 128 + 64],
                        scalar1=rden[:, h:h + 1], scalar2=None, op0=ALU.mult),
                        reads=on + ['rden'], writes=['otok'])

                def tp(e):
                    ins = None
                    for hc in range(4):
                        ins = e.matmul(bQA[:, hc * 128:(hc + 1) * 128], otok[:, hc * 128:(hc + 1) * 128],
                                       ident_bf[:], start=True, stop=True)
                    return ins
                P.op('pe', tp, reads=['otok', 'ident'], writes=rQA)
                P.op('act', lambda e, qb=qb, N=N: e.activation(
                    out=sview(oaT, N, 4)[:, :, qb * 128:(qb + 1) * 128],
                    in_=bQA[:, 0:512].rearrange("p (c n) -> p c n", c=4), func=AF.Copy),
                    reads=rQA, writes=['oaT'])
            P.dma('sp', 'oas', lambda e, t0=t0, N=N: e.dma_start(
                out=OA[:, t0:t0 + N].rearrange("(c p) n -> p c n", p=128), in_=sview(oaT, N, 4)),
                reads=['oaT'], writes=[f'OA:{T["idx"]}'])
        for T in tiles:
            do_tile(T)
        P.barrier()

    def mix_sweep(l, tiles, Xsrc, Xdst):
        W = Wl[l]
        A.reset(base_mark)
        PP.set_order([4, 5, 6, 7])
        Wu = A.alloc([128, 8 * 512], BF)
        Wvs = A.alloc([128, 8 * 512], BF)
        Wg = A.alloc([128, 8 * 2048], BF)
        Wpa = A.alloc([128, 4 * D], BF)
        Wpb = A.alloc([128, 4 * D], BF)
        Wo = A.alloc([128, 8 * D], BF)
        wsT = A.alloc([128, 512], BF)
        lnG = A.alloc([128, 512], F32)
        lnB = A.alloc([128, 512], F32)
        bsf = A.alloc([64, 512], F32)
        bsf2 = A.alloc([64, 512], F32)
        bsz = A.alloc([64, 512], BF)
        bg = A.alloc([128, 16], F32)
        hx = A.alloc([128, 8 * 512], BF)
        xt = A.alloc([128, 8 * 512], F32)
        oaT = A.alloc([128, 4 * 512], BF)
        u = A.alloc([128, 4 * 512], BF)
        ob = A.alloc([128, 4 * 512], BF)
        gates = A.alloc([128, 16 * 512], BF)
        m = A.alloc([128, 8 * 512], BF)
        vg = [A.alloc([128, 512], F32) for _ in range(2)]
        vnb = [A.alloc([128, 512], BF) for _ in range(2)]
        st6 = A.alloc([128, 6], F32)
        mv2 = A.alloc([128, 2], F32)
        sd = A.alloc([128, 1], F32)
        t1 = A.alloc([128, 512], F32)
        t2 = A.alloc([128, 512], F32)
        for kc in range(8):
            for (dst, c0, c1, wd) in ((Wu, 1536, 2048, 512), (Wvs, 2048, 2560, 512), (Wg, 2560, 4608, 2048)):
                P.dma('pool', 'wk', lambda e, kc=kc, dst=dst, c0=c0, c1=c1, wd=wd: e.dma_start(
                    out=dst[:, kc * wd:(kc + 1) * wd], in_=W['w_in'][kc * 128:(kc + 1) * 128, c0:c1]), writes=['Wm'])
            P.dma('pool', 'wk', lambda e, kc=kc: e.dma_start(
                out=Wo[:, kc * D:(kc + 1) * D], in_=W['w_o'][kc * 128:(kc + 1) * 128, :]), writes=['Wm'])
        for kc in range(4):
            P.dma('pool', 'wk', lambda e, kc=kc: e.dma_start(
                out=Wpa[:, kc * D:(kc + 1) * D], in_=W['w_pa'][kc * 128:(kc + 1) * 128, :]), writes=['Wm'])
            P.dma('pool', 'wk', lambda e, kc=kc: e.dma_start(
                out=Wpb[:, kc * D:(kc + 1) * D], in_=W['w_pb'][kc * 128:(kc + 1) * 128, :]), writes=['Wm'])
        P.dma('pool', 'wk', lambda e: e.dma_start(out=wsT[:], in_=W['wsT']), writes=['Wm'])
        P.dma('sp', 'c1', lambda e: e.dma_start(out=lnG[:], in_=W['lng']), writes=['ln'])
        P.dma('sp', 'c1', lambda e: e.dma_start(out=lnB[:], in_=W['lnb']), writes=['ln'])
        P.dma('sp', 'c3', lambda e: e.dma_start(out=bg[:], in_=W['bgT']), writes=['bg'])
        P.op('dve', lambda e: e.memset(bsf[:], 0.0), writes=['bsf'])
        P.op('dve', lambda e: e.memset(bsz[:], 0.0), writes=['bsz'])
        P.dma('sp', 'c2', lambda e: e.dma_start(out=bsf[0:1, :], in_=W['bs']), reads=['bsf'], writes=['bsf'])
        P.dma('sp', 'c2', lambda e: e.dma_start(out=bsf[32:33, :], in_=W['bs']), reads=['bsf'], writes=['bsf'])
        P.op('dve', lambda e: e.tensor_copy(out=bsz[0:1, :], in_=bsf[0:1, :]), reads=['bsf', 'bsz'], writes=['bsz'])
        P.op('dve', lambda e: e.tensor_copy(out=bsf2[32:33, :], in_=bsz[0:1, :]), reads=['bsz'], writes=['bsf2'])
        P.op('dve', lambda e: e.tensor_tensor(out=bsz[32:33, :], in0=bsf[32:33, :], in1=bsf2[32:33, :], op=ALU.subtract),
             reads=['bsf', 'bsf2', 'bsz'], writes=['bsz'])
        def do_tile(T):
            N, s, t0 = T['N'], T['s'], T['t0']
            nsub = N // 128
            P.dma('sp', 'hxl', lambda e, t0=t0, N=N: e.dma_start(out=sview(hx, N), in_=xview(HX, t0, N)), writes=['hx'])
            P.dma('sp', 'x0', lambda e, t0=t0, N=N: e.dma_start(out=sview(xt, N), in_=xview(Xsrc, t0, N)), writes=['xt'])
            P.dma('sp', 'oal', lambda e, t0=t0, N=N: e.dma_start(
                out=sview(oaT, N, 4), in_=OA[:, t0:t0 + N].rearrange("(c p) n -> p c n", p=128)), writes=['oaT'])
            for kc in range(4):
                bU, rU = PP.bank()

                def mu(e, kc=kc, bU=bU):
                    ins = None
                    for k in range(8):
                        ins = e.matmul(bU[:, 0:N], Wu[:, k * 512 + kc * 128: k * 512 + (kc + 1) * 128],
                                       hx[:, k * N:(k + 1) * N], start=(k == 0), stop=(k == 7))
                    return ins
                P.op('pe', mu, reads=['hx', 'Wm'], writes=[rU])
                P.op('act', lambda e, kc=kc, bU=bU: e.activation(out=u[:, kc * N:(kc + 1) * N], in_=bU[:, 0:N],
                                                                func=AF.Gelu_apprx_tanh),
                     reads=[rU], writes=['u'])
            for sub in range(nsub):
                bV, rV = PP.bank()
                vgb, vnbb = vg[sub % 2], vnb[sub % 2]
                rvg, rvn = f'vg{sub % 2}', f'vnb{sub % 2}'

                def mvs(e, sub=sub, bV=bV):
                    ins = None
                    for k in range(8):
                        ins = e.matmul(bV[:, 0:512], hx[:, k * N + sub * 128: k * N + (sub + 1) * 128],
                                       Wvs[:, k * 512:(k + 1) * 512], start=(k == 0), stop=(k == 7))
                    return ins
                P.op('pe', mvs, reads=['hx', 'Wm'], writes=[rV])
                P.op('act', lambda e, bV=bV, vgb=vgb: e.activation(out=vgb[:], in_=bV[:, 0:512], func=AF.Gelu_apprx_tanh),
                     reads=[rV], writes=[rvg])
                P.op('dve', lambda e, vgb=vgb: e.bn_stats(out=st6[:], in_=vgb[:]), reads=[rvg], writes=['st6'])
                P.op('dve', lambda e: e.bn_aggr(out=mv2[:], in_=st6[:]), reads=['st6'], writes=['mv2'])
                P.op('act', lambda e: e.activation(out=sd[:], in_=mv2[:, 1:2], func=AF.Sqrt, bias=eps_t[:], scale=1.0),
                     reads=['mv2', 'eps'], writes=['sd'])
                P.op('dve', lambda e: e.reciprocal(out=sd[:], in_=sd[:]), reads=['sd'], writes=['sd'])
                P.op('dve', lambda e, vgb=vgb: e.tensor_scalar(out=vgb[:], in0=vgb[:], scalar1=mv2[:, 0:1], scalar2=sd[:, 0:1],
                                                               op0=ALU.subtract, op1=ALU.mult),
                     reads=[rvg, 'mv2', 'sd'], writes=[rvg])
                P.op('dve', lambda e, vgb=vgb: e.tensor_tensor(out=vgb[:], in0=vgb[:], in1=lnG[:], op=ALU.mult),
                     reads=[rvg, 'ln'], writes=[rvg])
                P.op('pool', lambda e, vgb=vgb, vnbb=vnbb: e.tensor_tensor(out=vnbb[:], in0=vgb[:], in1=lnB[:], op=ALU.add),
                     reads=[rvg, 'ln'], writes=[rvn])

                def msg(e, sub=sub, vnbb=vnbb):
                    ins = None
                    for g in range(4):
                        o = ps[:, g * 512 + sub * 128: g * 512 + (sub + 1) * 128]
                        e.matmul(o, vnbb[:, g * 128:(g + 1) * 128], wsT[:, g * 128:(g + 1) * 128], start=True, stop=False)
                        ins = e.matmul(o, onesz[0:33, :], bsz[0:33, g * 128:(g + 1) * 128], start=False, stop=True)
                    return ins
                P.op('pe', msg, reads=[rvn, 'Wm', 'bsz', 'onesz'], writes=['psg'])
            for g in range(4):
                P.op('dve', lambda e, g=g: e.tensor_tensor(out=ob[:, g * N:(g + 1) * N], in0=ps[:, g * 512:g * 512 + N],
                                                           in1=u[:, g * N:(g + 1) * N], op=ALU.mult),
                     reads=['psg', 'u'], writes=['ob'])
            for gc in range(16):
                bG, rG = PP.bank()

                def mg(e, gc=gc, bG=bG):
                    ins = None
                    for k in range(8):
                        ins = e.matmul(bG[:, 0:N], Wg[:, k * 2048 + gc * 128: k * 2048 + (gc + 1) * 128],
                                       hx[:, k * N:(k + 1) * N], start=(k == 0), stop=(k == 7))
                    return ins
                P.op('pe', mg, reads=['hx', 'Wm'], writes=[rG])
                P.op('act', lambda e, gc=gc, bG=bG: e.activation(out=gates[:, gc * N:(gc + 1) * N], in_=bG[:, 0:N],
                                                                func=AF.Sigmoid, bias=bg[:, gc:gc + 1], scale=1.0),
                     reads=[rG, 'bg'], writes=['gates'])
            for oc in range(8):
                bA, rA = PP.bank()
                bB, rB = PP.bank()

                def mp(e, oc=oc, bank=bA, Wm=Wpa, src=oaT):
                    ins = None
                    for k in range(4):
                        ins = e.matmul(bank[:, 0:N], Wm[:, k * D + oc * 128: k * D + (oc + 1) * 128],
                                       src[:, k * N:(k + 1) * N], start=(k == 0), stop=(k == 3))
                    return ins
                P.op('pe', mp, reads=['oaT', 'Wm'], writes=[rA])
                P.op('pe', lambda e, oc=oc, bB=bB: mp(e, oc, bB, Wpb, ob), reads=['ob', 'Wm'], writes=[rB])
                P.op('dve', lambda e, oc=oc, bA=bA: e.tensor_tensor(out=t1[:, 0:N], in0=bA[:, 0:N],
                                                                   in1=gates[:, oc * N:(oc + 1) * N], op=ALU.mult),
                     reads=[rA, 'gates'], writes=['t1'])
                P.op('dve', lambda e, oc=oc, bB=bB: e.tensor_tensor(out=t2[:, 0:N], in0=bB[:, 0:N],
                                                                   in1=gates[:, (8 + oc) * N:(9 + oc) * N], op=ALU.mult),
                     reads=[rB, 'gates'], writes=['t2'])
                P.op('pool', lambda e, oc=oc: e.tensor_tensor(out=m[:, oc * N:(oc + 1) * N], in0=t1[:, 0:N], in1=t2[:, 0:N], op=ALU.add),
                     reads=['t1', 't2'], writes=['m'])
            for oc in range(8):
                bY, rY = PP.bank()

                def mo(e, oc=oc, bY=bY):
                    ins = None
                    for k in range(8):
                        ins = e.matmul(bY[:, 0:N], Wo[:, k * D + oc * 128: k * D + (oc + 1) * 128],
                                       m[:, k * N:(k + 1) * N], start=(k == 0), stop=(k == 7))
                    return ins
                P.op('pe', mo, reads=['m', 'Wm'], writes=[rY])
                P.op('dve', lambda e, oc=oc, bY=bY: e.scalar_tensor_tensor(
                    out=xt[:, oc * N:(oc + 1) * N], in0=bY[:, 0:N], scalar=mvcol(5, oc, s),
                    in1=xt[:, oc * N:(oc + 1) * N], op0=ALU.mult, op1=ALU.add),
                    reads=[rY, 'xt', 'mv'], writes=['xt'])
            P.dma('sp', 'xs0', lambda e, t0=t0, N=N: e.dma_start(out=xview(Xdst, t0, N), in_=sview(xt, N)),
                  reads=['xt'], writes=[f'XD:{T["idx"]}'])
        for T in tiles:
            do_tile(T)
        P.barrier()

    allx = xtiles(0, NT)
    own = xtiles(HALO // 4, NT - HALO // 4)
    for l in range(DEPTH):
        last = (l == DEPTH - 1)
        Xin = xT_in if l == 0 else XS[2]
        layer_setup(l)
        ffn_sweep(l, 0, allx + [CT], Xin, XS[0])
        kv_sweep(l, allx + [CT], XS[0])
        if not last:
            att_sweep(l, allx + [CT])
            mix_sweep(l, allx + [CT], XS[0], XS[1])
            ffn_sweep(l, 2, allx + [CT], XS[1], XS[2])
        else:
            att_sweep(l, own)
            mix_sweep(l, own, XS[0], XS[1])
            ffn_sweep(l, 2, own, XS[1], None, final=True)

    with nc.Block() as block:
        P.emit(block)
    return nc


def _fm(v):
    v = np.asarray(v, np.float32)
    return np.ascontiguousarray(v.reshape(-1, 128).T)


def _bias_tables(rpb_l, j, NPG):
    rows = 2 * NPG
    kk = np.arange(128)
    qq = np.arange(128)

    def table(gq, offs, npad):
        out = np.full((128, NH, npad, 128), NEG, np.float32)
        qr = 2 * gq + qq // 64
        qc = qq % 64
        rs = np.clip(qr - 4, 0, rows - 8)
        cs = np.clip(qc - 8, 0, GW - 16)
        for oi, o in enumerate(offs):
            gk = gq + o
            if gk < 0 or gk >= NPG:
                continue
            kr = 2 * gk + kk // 64
            kc = kk % 64
            valid = ((kr[:, None] >= rs[None, :]) & (kr[:, None] < rs[None, :] + 8) &
                     (kc[:, None] >= cs[None, :]) & (kc[:, None] < cs[None, :] + 16))
            dr = np.clip(kr[:, None] - qr[None, :] + 7, 0, 14)
            dc = np.clip(kc[:, None] - qc[None, :] + 15, 0, 30)
            g = rpb_l[:, dr, dc]
            g = np.where(valid[None], g, np.float32(NEG))
            out[:, :, oi, :] = np.transpose(g, (1, 0, 2))
        return out.reshape(128, -1)

    gen = table(NPG // 2, [-2, -1, 0, 1, 2], 5)
    g0 = j * NP_OWN
    sp = [table(g0 + 0, [-2, -1, 0, 1, 2, 3], 6),
          table(g0 + 1, [-2, -1, 0, 1, 2], 6),
          table(g0 + NP_OWN - 2, [-2, -1, 0, 1, 2], 6),
          table(g0 + NP_OWN - 1, [-3, -2, -1, 0, 1, 2], 6)]
    return np.ascontiguousarray(np.concatenate([gen] + sp, axis=1))


def _rope_tables(j, NPG, NS):
    d = np.arange(128) % 64
    i = d // 2
    par = d % 2
    n_freq = HD // 4
    freqs = (np.float32(10000.0) ** (-np.arange(n_freq, dtype=np.float32) / np.float32(n_freq))).astype(np.float32)
    t = np.arange(NS * 128)
    slot = t // 128
    g = j * NP_OWN - HALO + slot
    row = (2 * g + (t % 128) // 64).astype(np.float32)
    col = (t % 64).astype(np.float32)
    fr = freqs[i % n_freq]
    ang = np.where((i < n_freq)[:, None], row[None, :] * fr[:, None], col[None, :] * fr[:, None]).astype(np.float32)
    cos = np.cos(ang).astype(np.float32)
    sin = np.sin(ang).astype(np.float32)
    sins = np.where((par == 0)[:, None], -sin, sin).astype(np.float32)
    return np.ascontiguousarray(cos), np.ascontiguousarray(sins)


_NC_CACHE = {}


def kernel(x, c, ctx, c_ctx, w_ada, b_ada, norm_g, w_ff1_up, w_ff1_down, w_in, b_gate,
           rpb, ln_v_g, ln_v_b, w_s, b_s, w_pa, w_pb, w_o, w_ff2_up, w_ff2_down, final_g):
    C = cfg()
    NS, TX, TALL, NPG = C['NS'], C['TX'], C['TALL'], C['NPG']
    f = lambda a: np.ascontiguousarray(np.asarray(a, np.float32))
    x, c, ctx, c_ctx = f(x), f(c), f(ctx), f(c_ctx)
    B = x.shape[0]
    assert x.shape[1] == NPG * 128 and B * CPB == NCORE
    swap = np.arange(512) ^ 1
    shared = {"ident": np.eye(128, dtype=np.float32), "fgT": _fm(final_g)}
    for l in range(DEPTH):
        wi = f(w_in[l])
        shared[f"w_ada{l}"] = f(w_ada[l])
        shared[f"b_adaT{l}"] = _fm(b_ada[l])
        shared[f"ngT{l}"] = _fm(np.asarray(norm_g[l]).reshape(-1))
        shared[f"up1_{l}"] = f(w_ff1_up[l]); shared[f"dn1_{l}"] = f(w_ff1_down[l])
        shared[f"up2_{l}"] = f(w_ff2_up[l]); shared[f"dn2_{l}"] = f(w_ff2_down[l])
        shared[f"w_in{l}"] = wi
        shared[f"w_sw{l}"] = np.ascontiguousarray(np.concatenate([wi[:, 0:512][:, swap], wi[:, 512:1024][:, swap]], axis=1))
        shared[f"bgT{l}"] = _fm(b_gate[l])
        shared[f"lng{l}"] = np.ascontiguousarray(np.broadcast_to(f(ln_v_g[l])[None, :], (128, 512)))
        shared[f"lnb{l}"] = np.ascontiguousarray(np.broadcast_to(f(ln_v_b[l])[None, :], (128, 512)))
        shared[f"wsT{l}"] = np.ascontiguousarray(np.transpose(f(w_s[l]), (2, 0, 1)).reshape(128, 512))
        shared[f"bs{l}"] = f(b_s[l]).reshape(1, 512)
        shared[f"w_pa{l}"] = f(w_pa[l]); shared[f"w_pb{l}"] = f(w_pb[l]); shared[f"w_o{l}"] = f(w_o[l])
    rpb = f(rpb)
    in_maps = []
    for core in range(NCORE):
        b, j = core // CPB, core % CPB
        xT = np.zeros((D, TALL), np.float32)
        g_lo = j * NP_OWN - HALO
        v_lo, v_hi = max(g_lo, 0), min(g_lo + NS, NPG)
        xT[:, (v_lo - g_lo) * 128:(v_hi - g_lo) * 128] = x[b, v_lo * 128:v_hi * 128, :].T
        xT[:, TX:] = ctx[b].T
        cosT, sinT = _rope_tables(j, NPG, NS)
        cond = np.stack([_fm(c[b]), _fm(c_ctx)], axis=-1).reshape(128, 16)
        m = dict(shared)
        m.update({"xT": xT, "cosT": cosT, "sinT": sinT, "condT": np.ascontiguousarray(cond)})
        for l in range(DEPTH):
            m[f"bt{l}"] = _bias_tables(rpb[l], j, NPG)
        in_maps.append(m)
    key = (NP_OWN, DEBUG_OUT)
    if key not in _NC_CACHE:
        _NC_CACHE[key] = build_program()
    nc = _NC_CACHE[key]
    res = run_bass_kernel_spmd(nc, in_maps, core_ids=list(range(NCORE)))
    out = np.zeros((B, NPG * 128, D), np.float32)
    for core in range(NCORE):
        b, j = core // CPB, core % CPB
        out[b, j * NP_OWN * 128:(j + 1) * NP_OWN * 128, :] = np.asarray(res.results[core]["yT"], np.float32).T
    if DEBUG_OUT:
        kernel.last_results = res.results
    return out
```

```python
import numpy as np
import concourse.bass as bass
import concourse.mybir as mybir
from concourse.bass_utils import run_bass_kernel_spmd

F32 = mybir.dt.float32
BF = mybir.dt.bfloat16
AF = mybir.ActivationFunctionType
ALU = mybir.AluOpType

D = 1024
DFF = 2816
NH = 8
HD = 64
CTX = 256
GW = 64
NCORE = 8
CPB = 4
NP_OWN = 32
HALO = 4
DEPTH = 2
EPS = 1e-6
NEG = -30000.0
DEBUG_OUT = False

ENGS = ['pe', 'act', 'dve', 'pool', 'sp']
BLK_ATTR = {'pe': 'tensor', 'act': 'scalar', 'dve': 'vector', 'pool': 'gpsimd', 'sp': 'sync'}


def cfg():
    ns = NP_OWN + 2 * HALO
    tx = ns * 128
    return dict(NS=ns, TX=tx, TALL=tx + CTX, NT=ns // 4, NPG=CPB * NP_OWN)


class Prog:
    def __init__(self, nc):
        self.nc = nc
        self.ops = {e: [] for e in ENGS}
        self.cnt = {e: 0 for e in ENGS}
        self.dcnt = {}
        self.lastw = {}
        self.readers = {}
        self.waited = {e: {} for e in ENGS}

    def _deps(self, eng, reads, writes):
        toks = []
        for r in reads:
            if r in self.lastw:
                toks.append(self.lastw[r])
        for w in writes:
            if w in self.lastw:
                toks.append(self.lastw[w])
            toks += self.readers.get(w, [])
        need = {}
        for sem, val in toks:
            if sem == eng and eng == 'pe':
                continue
            if self.waited[eng].get(sem, 0) >= val:
                continue
            need[sem] = max(need.get(sem, 0), val)
        for sem, val in need.items():
            self.waited[eng][sem] = val
        return list(need.items())

    def _commit(self, tok, reads, writes):
        for r in reads:
            self.readers.setdefault(r, []).append(tok)
        for w in writes:
            self.lastw[w] = tok
            self.readers[w] = []

    def op(self, eng, fn, reads=(), writes=()):
        waits = self._deps(eng, reads, writes)
        self.cnt[eng] += 1
        tok = (eng, self.cnt[eng])
        self.ops[eng].append((waits, fn, eng, 1))
        self._commit(tok, reads, writes)

    def dma(self, q, key, fn, reads=(), writes=()):
        waits = self._deps(q, reads, writes)
        self.dcnt[key] = self.dcnt.get(key, 0) + 16
        tok = ('d:' + key, self.dcnt[key])
        self.ops[q].append((waits, fn, tok[0], 16))
        self._commit(tok, reads, writes)

    def barrier(self):
        allt = [(e, self.cnt[e]) for e in ENGS if self.cnt[e] > 0]
        allt += [('d:' + k, v) for k, v in self.dcnt.items()]
        for e in ENGS:
            need = [(sem, v) for sem, v in allt if self.waited[e].get(sem, 0) < v]
            for sem, v in need:
                self.waited[e][sem] = v
            self.ops[e].append((need, None, None, 0))
        self.lastw = {}
        self.readers = {}

    def emit(self, block):
        nc = self.nc
        names = [e for e in ENGS] + ['d:' + k for k in self.dcnt]
        sems = {n: nc.alloc_semaphore('s_' + n.replace(':', '_')) for n in names}
        for e in ENGS:
            ops = self.ops[e]

            def body(eng, ops=ops):
                for waits, fn, semname, inc in ops:
                    for sem, v in waits:
                        eng.wait_ge(sems[sem], v)
                    if fn is not None:
                        ins = fn(eng)
                        ins.then_inc(sems[semname], inc)
            getattr(block, BLK_ATTR[e])(body)


class Arena:
    def __init__(self, nc):
        self.nc = nc
        self.base = ((nc.sbuf_base + 63) // 64) * 64
        self.top = nc.sbuf_top
        self.p = self.base
        self.n = 0

    def alloc(self, shape, dtype):
        esz = 4 if dtype == F32 else 2
        nb = esz
        for s in shape[1:]:
            nb *= s
        nb = ((nb + 63) // 64) * 64
        off = self.p
        self.p += nb
        assert self.p <= self.top, f"SBUF overflow {self.p} > {self.top}"
        self.n += 1
        return self.nc.alloc_sbuf_tensor_at(f"sb{self.n}", list(shape), dtype, offset=off)

    def mark(self):
        return self.p

    def reset(self, m):
        self.p = m


class PsumPool:
    def __init__(self, ps):
        self.ps = ps
        self.order = list(range(8))
        self.i = 0

    def set_order(self, order):
        self.order = list(order)
        self.i = 0

    def bank(self):
        b = self.order[self.i % len(self.order)]
        self.i += 1
        return self.ps[:, b * 512:(b + 1) * 512], f"ps{b}"

    def fixed(self, b, nb=1):
        return self.ps[:, b * 512:(b + nb) * 512], [f"ps{b + k}" for k in range(nb)]


def build_program():
    C = cfg()
    NS, TX, TALL, NT = C['NS'], C['TX'], C['TALL'], C['NT']
    nc = bass.Bass("TRN2", target_bir_lowering=False)

    def din(name, shape, dt=F32):
        return nc.dram_tensor(name, list(shape), dt, kind="ExternalInput").ap()

    def dscr(name, shape, dt):
        kind = "ExternalOutput" if DEBUG_OUT else "Internal"
        return nc.dram_tensor(name, list(shape), dt, kind=kind).ap()

    xT_in = din("xT", [D, TALL])
    cosT = din("cosT", [128, TX])
    sinT = din("sinT", [128, TX])
    condT = din("condT", [128, 16])
    ident_d = din("ident", [128, 128])
    fgT = din("fgT", [128, 8])
    Wl = []
    for l in range(DEPTH):
        Wl.append(dict(
            w_ada=din(f"w_ada{l}", [D, 9 * D]),
            b_adaT=din(f"b_adaT{l}", [128, 72]),
            ngT=din(f"ngT{l}", [128, 24]),
            up1=din(f"up1_{l}", [D, 2 * DFF]), dn1=din(f"dn1_{l}", [DFF, D]),
            up2=din(f"up2_{l}", [D, 2 * DFF]), dn2=din(f"dn2_{l}", [DFF, D]),
            w_in=din(f"w_in{l}", [D, 4608]),
            w_sw=din(f"w_sw{l}", [D, 1024]),
            bgT=din(f"bgT{l}", [128, 16]),
            bt=din(f"bt{l}", [128, 8 * 5 * 128 + 4 * 8 * 6 * 128]),
            lng=din(f"lng{l}", [128, 512]), lnb=din(f"lnb{l}", [128, 512]),
            wsT=din(f"wsT{l}", [128, 512]),
            bs=din(f"bs{l}", [1, 512]),
            w_pa=din(f"w_pa{l}", [512, D]), w_pb=din(f"w_pb{l}", [512, D]),
            w_o=din(f"w_o{l}", [D, D]),
        ))
    yT = nc.dram_tensor("yT", [D, NP_OWN * 128], F32, kind="ExternalOutput").ap()
    XS = [dscr(f"xs{k}", [D, TALL], F32) for k in range(3)]
    HX = dscr("hxs", [D, TALL], BF)
    KS = dscr("kss", [512, TALL], BF)
    VS = dscr("vss", [NS + 2, 128, 520], BF)
    OA = dscr("oas", [512, TALL], BF)

    ps_t = nc.alloc_psum_tensor("psall", [128, 4096], F32)
    ps = ps_t[:]
    PP = PsumPool(ps)
    A = Arena(nc)
    P = Prog(nc)

    ones_bf = A.alloc([128, 128], BF)
    ident_bf = A.alloc([128, 128], BF)
    eps_t = A.alloc([128, 1], F32)
    onesz = A.alloc([64, 128], BF)
    cond_f = A.alloc([128, 16], F32)
    cond_bf = A.alloc([128, 16], BF)
    fg_t = A.alloc([128, 8], F32)
    mv = A.alloc([128, 144], F32)
    GS = A.alloc([128, 48], F32)
    HG = A.alloc([128, 32], F32)
    b_ada_t = A.alloc([128, 72], F32)
    ng_t = A.alloc([128, 24], F32)

    P.op('dve', lambda e: e.memset(ones_bf[:], 1.0), writes=['ones'])
    P.op('dve', lambda e: e.memset(eps_t[:], EPS), writes=['eps'])
    P.op('dve', lambda e: e.memset(onesz[:], 0.0), writes=['onesz'])
    P.op('dve', lambda e: e.memset(onesz[0:1, :], 1.0), writes=['onesz'])
    P.op('dve', lambda e: e.memset(onesz[32:33, :], 1.0), writes=['onesz'])
    P.dma('pool', 'c0', lambda e: e.dma_start(out=ident_bf[:], in_=ident_d), writes=['ident'])
    P.dma('sp', 'c1', lambda e: e.dma_start(out=cond_f[:], in_=condT), writes=['condf'])
    P.dma('sp', 'c3', lambda e: e.dma_start(out=fg_t[:], in_=fgT), writes=['fg'])
    P.op('act', lambda e: e.activation(out=cond_bf[:], in_=cond_f[:], func=AF.Silu),
         reads=['condf'], writes=['condb'])
    P.barrier()
    base_mark = A.mark()

    def mvcol(idx, c, s):
        k = ((idx * 8 + c) * 2 + s)
        return mv[:, k:k + 1]

    def gscol(i, c, s):
        k = ((i * 8 + c) * 2 + s)
        return GS[:, k:k + 1]

    def hgcol(kk, c, s):
        k = ((kk * 8 + c) * 2 + s)
        return HG[:, k:k + 1]

    def xtiles(lo, hi):
        return [dict(t0=t * 512, N=512, s=0, idx=t) for t in range(lo, hi)]
    CT = dict(t0=TX, N=CTX, s=1, idx=NT)

    def wload(dst, src, key, res, nsplit, rows_per, cols):
        P.dma('pool', key,
              lambda e: e.dma_start(out=dst[:, 0:nsplit * cols].rearrange("p (k c) -> p k c", k=nsplit),
                                    in_=src.rearrange("(k p) c -> p k c", p=128)),
              writes=[res])

    def layer_setup(l):
        W = Wl[l]
        A.reset(base_mark)
        wp = [A.alloc([128, 8 * 1024], BF) for _ in range(2)]
        P.dma('sp', 'c1', lambda e: e.dma_start(out=b_ada_t[:], in_=W['b_adaT']), writes=['bada'])
        P.dma('sp', 'c3', lambda e: e.dma_start(out=ng_t[:], in_=W['ngT']), writes=['ng'])
        psm, psr = PP.fixed(0)
        for i in range(9):
            b = i % 2
            wload(wp[b], W['w_ada'][:, i * 1024:(i + 1) * 1024], f'wp{b}', f'wp{b}', 8, 128, 1024)

            def mm(e, i=i, b=b):
                ins = None
                for c in range(8):
                    for kc in range(8):
                        col = (i * 8 + c) * 2
                        ins = e.matmul(psm[:, col:col + 2],
                                       wp[b][:, kc * 1024 + c * 128: kc * 1024 + (c + 1) * 128],
                                       cond_bf[:, kc * 2:kc * 2 + 2],
                                       start=(kc == 0), stop=(kc == 7))
                return ins
            P.op('pe', mm, reads=[f'wp{b}', 'condb'], writes=psr)
        for s in range(2):
            P.op('dve', lambda e, s=s: e.tensor_tensor(
                out=mv[:, s:144:2], in0=psm[:, s:144:2], in1=b_ada_t[:], op=ALU.add),
                reads=psr + ['bada'], writes=['mv'])
        for i in range(3):
            for s in range(2):
                a0 = ((3 * i + 1) * 8) * 2 + s
                g0 = (i * 8) * 2 + s
                P.op('dve', lambda e, a0=a0, g0=g0, i=i: e.scalar_tensor_tensor(
                    out=GS[:, g0:g0 + 15:2], in0=mv[:, a0:a0 + 15:2], scalar=1.0,
                    in1=ng_t[:, i * 8:(i + 1) * 8], op0=ALU.add, op1=ALU.mult),
                    reads=['mv', 'ng'], writes=['GS'])
        for kk, i in enumerate((0, 2)):
            for s in range(2):
                a0 = ((3 * i + 2) * 8) * 2 + s
                h0 = (kk * 8) * 2 + s
                P.op('dve', lambda e, a0=a0, h0=h0: e.tensor_scalar(
                    out=HG[:, h0:h0 + 15:2], in0=mv[:, a0:a0 + 15:2], scalar1=0.5, scalar2=None,
                    op0=ALU.mult), reads=['mv'], writes=['HG'])
        P.barrier()

    def norm_ops(xt, rx, hx, rh, N, i, s, rstd, tmp):
        rhs_names = [f'{rh}{c}' for c in range(8)]
        P.op('act', lambda e: e.activation(out=hx[:, 0:8 * N], in_=xt[:, 0:8 * N], func=AF.Square),
             reads=[rx], writes=rhs_names)
        bk, br = PP.bank()

        def mm(e):
            ins = None
            for c in range(8):
                ins = e.matmul(bk[:, 0:N], ones_bf[:], hx[:, c * N:(c + 1) * N],
                               start=(c == 0), stop=(c == 7))
            return ins
        P.op('pe', mm, reads=rhs_names + ['ones'], writes=[br])
        P.op('act', lambda e: e.activation(out=rstd[:, 0:N], in_=bk[:, 0:N], func=AF.Sqrt,
                                           bias=eps_t[:], scale=1.0 / D),
             reads=[br, 'eps'], writes=['rstd'])
        P.op('dve', lambda e: e.reciprocal(out=rstd[:, 0:N], in_=rstd[:, 0:N]),
             reads=['rstd'], writes=['rstd'])
        for c in range(8):
            tb = tmp[c % 2]
            P.op('dve', lambda e, c=c, tb=tb: e.scalar_tensor_tensor(
                out=tb[:, 0:N], in0=xt[:, c * N:(c + 1) * N], scalar=gscol(i, c, s),
                in1=rstd[:, 0:N], op0=ALU.mult, op1=ALU.mult),
                reads=[rx, 'rstd', 'GS'], writes=[f'tmp{c % 2}'])
            P.op('act', lambda e, c=c, tb=tb: e.activation(
                out=hx[:, c * N:(c + 1) * N], in_=tb[:, 0:N], func=AF.Identity,
                bias=mvcol(3 * i, c, s), scale=1.0),
                reads=[f'tmp{c % 2}', 'mv'], writes=[f'{rh}{c}'])
        return rhs_names

    def xview(dram, t0, N, nchunk=8):
        return dram[:, t0:t0 + N].rearrange("(c p) n -> p c n", p=128)

    def sview(t, N, nchunk=8):
        return t[:, 0:nchunk * N].rearrange("p (c n) -> p c n", c=nchunk)

    def ffn_sweep(l, i, tiles, Xsrc, Xdst, final=False):
        W = Wl[l]
        kk = 0 if i == 0 else 1
        A.reset(base_mark)
        PP.set_order(range(8))
        Wup = A.alloc([128, 8 * 2 * DFF], BF)
        Wdn = A.alloc([128, 22 * D], BF)
        xt = [A.alloc([128, 8 * 512], F32) for _ in range(2)]
        hx = A.alloc([128, 8 * 512], BF)
        act = A.alloc([128, 22 * 512], BF)
        rstd = A.alloc([128, 512], F32)
        tmp = [A.alloc([128, 512], F32) for _ in range(2)]
        sa = [A.alloc([128, 512], F32) for _ in range(2)]
        wload(Wup, W['up1' if i == 0 else 'up2'], 'wup', 'Wup', 8, 128, 2 * DFF)
        wload(Wdn, W['dn1' if i == 0 else 'dn2'], 'wdn', 'Wdn', 22, 128, D)

        def load(ti, b):
            T = tiles[ti]
            P.dma('sp', f'x{b}', lambda e: e.dma_start(out=sview(xt[b], T['N']),
                                                       in_=xview(Xsrc, T['t0'], T['N'])),
                  reads=[f"{id(Xsrc)}:{T['idx']}"], writes=[f'xt{b}'])
        load(0, 0)

        def do_tile(ti, T):
            b = ti % 2
            N, s = T['N'], T['s']
            if ti + 1 < len(tiles):
                load(ti + 1, 1 - b)
            x = xt[b]
            rx = f'xt{b}'
            hn = norm_ops(x, rx, hx, 'hx', N, i, s, rstd, tmp)
            for j in range(22):
                bA, rA = PP.bank()
                bB, rB = PP.bank()

                def mmu(e, j=j, bank=bA, off=0):
                    ins = None
                    for kc in range(8):
                        c0 = kc * 2 * DFF + off + j * 128
                        ins = e.matmul(bank[:, 0:N], Wup[:, c0:c0 + 128], hx[:, kc * N:(kc + 1) * N],
                                       start=(kc == 0), stop=(kc == 7))
                    return ins
                P.op('pe', mmu, reads=hn + ['Wup'], writes=[rA])
                P.op('pe', lambda e, j=j, bank=bB: mmu(e, j, bank, DFF), reads=hn + ['Wup'], writes=[rB])
                sb = sa[j % 2]
                P.op('act', lambda e, bA=bA, sb=sb: e.activation(out=sb[:, 0:N], in_=bA[:, 0:N], func=AF.Silu),
                     reads=[rA], writes=[f'sa{j % 2}'])
                P.op('dve', lambda e, bB=bB, sb=sb, j=j: e.tensor_tensor(
                    out=act[:, j * N:(j + 1) * N], in0=bB[:, 0:N], in1=sb[:, 0:N], op=ALU.mult),
                    reads=[rB, f'sa{j % 2}'], writes=[f'act{j}'])
            an = [f'act{j}' for j in range(22)]
            for oc in range(8):
                bY, rY = PP.bank()

                def mmd(e, oc=oc, bY=bY):
                    ins = None
                    for j in range(22):
                        ins = e.matmul(bY[:, 0:N], Wdn[:, j * D + oc * 128: j * D + (oc + 1) * 128],
                                       act[:, j * N:(j + 1) * N], start=(j == 0), stop=(j == 21))
                    return ins
                P.op('pe', mmd, reads=an + ['Wdn'], writes=[rY])
                P.op('dve', lambda e, oc=oc, bY=bY: e.scalar_tensor_tensor(
                    out=x[:, oc * N:(oc + 1) * N], in0=bY[:, 0:N], scalar=hgcol(kk, oc, s),
                    in1=x[:, oc * N:(oc + 1) * N], op0=ALU.mult, op1=ALU.add),
                    reads=[rY, rx, 'HG'], writes=[rx])
            if final:
                hn2 = [f'hx{c}' for c in range(8)]
                P.op('act', lambda e: e.activation(out=hx[:, 0:8 * N], in_=x[:, 0:8 * N], func=AF.Square),
                     reads=[rx], writes=hn2)
                bk, br = PP.bank()

                def mm2(e, bk=bk):
                    ins = None
                    for c in range(8):
                        ins = e.matmul(bk[:, 0:N], ones_bf[:], hx[:, c * N:(c + 1) * N],
                                       start=(c == 0), stop=(c == 7))
                    return ins
                P.op('pe', mm2, reads=hn2 + ['ones'], writes=[br])
                P.op('act', lambda e, bk=bk: e.activation(out=rstd[:, 0:N], in_=bk[:, 0:N], func=AF.Sqrt,
                                                         bias=eps_t[:], scale=1.0 / D),
                     reads=[br, 'eps'], writes=['rstd'])
                P.op('dve', lambda e: e.reciprocal(out=rstd[:, 0:N], in_=rstd[:, 0:N]),
                     reads=['rstd'], writes=['rstd'])
                for c in range(8):
                    P.op('dve', lambda e, c=c: e.scalar_tensor_tensor(
                        out=x[:, c * N:(c + 1) * N], in0=x[:, c * N:(c + 1) * N], scalar=fg_t[:, c:c + 1],
                        in1=rstd[:, 0:N], op0=ALU.mult, op1=ALU.mult),
                        reads=[rx, 'rstd', 'fg'], writes=[rx])
                o0 = T['t0'] - HALO * 128
                P.dma('sp', f'xs{b}', lambda e, o0=o0: e.dma_start(
                    out=yT[:, o0:o0 + N].rearrange("(c p) n -> p c n", p=128), in_=sview(x, N)),
                    reads=[rx], writes=[f"y:{T['idx']}"])
            else:
                P.dma('sp', f'xs{b}', lambda e: e.dma_start(out=xview(Xdst, T['t0'], N), in_=sview(x, N)),
                      reads=[rx], writes=[f"{id(Xdst)}:{T['idx']}"])
        for ti, T in enumerate(tiles):
            do_tile(ti, T)
        P.barrier()

    def kv_sweep(l, tiles, Xsrc):
        W = Wl[l]
        A.reset(base_mark)
        PP.set_order(range(8))
        Wk = A.alloc([128, 8 * 512], BF)
        Wks = A.alloc([128, 8 * 512], BF)
        Wv = A.alloc([128, 8 * 512], BF)
        xt = [A.alloc([128, 8 * 512], F32) for _ in range(2)]
        hx = A.alloc([128, 8 * 512], BF)
        cs_t = A.alloc([128, 512], F32)
        sn_t = A.alloc([128, 512], F32)
        kt = A.alloc([128, 4 * 512], BF)
        rstd = A.alloc([128, 512], F32)
        tmp = [A.alloc([128, 512], F32) for _ in range(2)]
        t1 = A.alloc([128, 512], F32)
        t2 = A.alloc([128, 512], F32)
        va = A.alloc([128, 4 * 520], BF)
        wload(Wk, W['w_in'][:, 512:1024], 'wk', 'Wk', 8, 128, 512)
        wload(Wks, W['w_sw'][:, 512:1024], 'wk', 'Wk', 8, 128, 512)
        wload(Wv, W['w_in'][:, 1024:1536], 'wk', 'Wk', 8, 128, 512)
        P.op('dve', lambda e: e.memset(va[:], 1.0), writes=['va'])

        def load(ti, b):
            T = tiles[ti]
            P.dma('sp', f'x{b}', lambda e: e.dma_start(out=sview(xt[b], T['N']),
                                                       in_=xview(Xsrc, T['t0'], T['N'])),
                  writes=[f'xt{b}'])
        load(0, 0)

        def do_tile(ti, T):
            b = ti % 2
            N, s, t0 = T['N'], T['s'], T['t0']
            if ti + 1 < len(tiles):
                load(ti + 1, 1 - b)
            if s == 0:
                P.dma('sp', 'cs', lambda e, t0=t0: e.dma_start(out=cs_t[:], in_=cosT[:, t0:t0 + 512]), writes=['cs'])
                P.dma('sp', 'cs', lambda e, t0=t0: e.dma_start(out=sn_t[:], in_=sinT[:, t0:t0 + 512]), writes=['cs'])
            x = xt[b]
            hn = norm_ops(x, f'xt{b}', hx, 'hx', N, 1, s, rstd, tmp)
            P.dma('sp', 'hxs', lambda e, t0=t0, N=N: e.dma_start(out=xview(HX, t0, N), in_=sview(hx, N)),
                  reads=hn, writes=[f'HX:{T["idx"]}'])
            for kc in range(4):
                bA, rA = PP.bank()

                def mk(e, kc=kc, bank=bA, Wm=Wk):
                    ins = None
                    for k in range(8):
                        ins = e.matmul(bank[:, 0:N], Wm[:, k * 512 + kc * 128: k * 512 + (kc + 1) * 128],
                                       hx[:, k * N:(k + 1) * N], start=(k == 0), stop=(k == 7))
                    return ins
                P.op('pe', mk, reads=hn + ['Wk'], writes=[rA])
                if s == 0:
                    bB, rB = PP.bank()
                    P.op('pe', lambda e, kc=kc, bank=bB: mk(e, kc, bank, Wks), reads=hn + ['Wk'], writes=[rB])
                    P.op('dve', lambda e, bA=bA: e.tensor_tensor(out=t1[:, 0:N], in0=bA[:, 0:N], in1=cs_t[:, 0:N], op=ALU.mult),
                         reads=[rA, 'cs'], writes=['t1'])
                    P.op('dve', lambda e, bB=bB: e.tensor_tensor(out=t2[:, 0:N], in0=bB[:, 0:N], in1=sn_t[:, 0:N], op=ALU.mult),
                         reads=[rB, 'cs'], writes=['t2'])
                    P.op('pool', lambda e, kc=kc: e.tensor_tensor(out=kt[:, kc * N:(kc + 1) * N], in0=t1[:, 0:N], in1=t2[:, 0:N], op=ALU.add),
                         reads=['t1', 't2'], writes=['kt'])
                else:
                    P.op('act', lambda e, kc=kc, bA=bA: e.activation(out=kt[:, kc * N:(kc + 1) * N], in_=bA[:, 0:N], func=AF.Copy),
                         reads=[rA], writes=['kt'])
            P.dma('sp', 'kts', lambda e, t0=t0, N=N: e.dma_start(
                out=KS[:, t0:t0 + N].rearrange("(c p) n -> p c n", p=128), in_=sview(kt, N, 4)),
                reads=['kt'], writes=[f'KS:{T["idx"]}'])
            nsub = N // 128
            for sub in range(nsub):
                bV, rV = PP.bank()

                def mv_(e, sub=sub, bV=bV):
                    ins = None
                    for k in range(8):
                        ins = e.matmul(bV[:, 0:512], hx[:, k * N + sub * 128: k * N + (sub + 1) * 128],
                                       Wv[:, k * 512:(k + 1) * 512], start=(k == 0), stop=(k == 7))
                    return ins
                P.op('pe', mv_, reads=hn + ['Wk'], writes=[rV])
                P.op('act', lambda e, sub=sub, bV=bV: e.activation(
                    out=va[:, sub * 520:(sub + 1) * 520].rearrange("p (h e) -> p h e", e=65)[:, :, 0:64],
                    in_=bV[:, 0:512].rearrange("p (h e) -> p h e", e=64), func=AF.Copy),
                    reads=[rV], writes=['va'])
            sl0 = t0 // 128
            P.dma('sp', 'vas', lambda e, sl0=sl0, nsub=nsub: e.dma_start(
                out=VS[sl0:sl0 + nsub].rearrange("s p e -> p s e"),
                in_=va[:, 0:nsub * 520].rearrange("p (s e) -> p s e", e=520)),
                reads=['va'], writes=[f'VS:{T["idx"]}'])
        for ti, T in enumerate(tiles):
            do_tile(ti, T)
        P.barrier()

    def att_sweep(l, tiles):
        W = Wl[l]
        A.reset(base_mark)
        Wq = A.alloc([128, 8 * 512], BF)
        Wqs = A.alloc([128, 8 * 512], BF)
        Eg = A.alloc([128, 8 * 5 * 128], BF)
        Es = [A.alloc([128, 8 * 6 * 128], BF) for _ in range(4)]
        stg = A.alloc([128, 8 * 6 * 128], F32)
        Kc = A.alloc([128, 4 * 256], BF)
        Vc = A.alloc([128, 2 * 520], BF)
        hx = A.alloc([128, 8 * 512], BF)
        cs_t = A.alloc([128, 512], F32)
        sn_t = A.alloc([128, 512], F32)
        Q = A.alloc([128, 4 * 512], BF)
        t1 = A.alloc([128, 512], F32)
        t2 = A.alloc([128, 512], F32)
        Kb = [A.alloc([128, 4 * 768], BF) for _ in range(2)]
        Vb = [A.alloc([128, 6 * 520], BF) for _ in range(2)]
        Pb = [A.alloc([128, 1024], BF) for _ in range(2)]
        otok = A.alloc([128, 512], BF)
        rden = A.alloc([128, 8], F32)
        oaT = A.alloc([128, 4 * 512], BF)
        wload(Wq, W['w_in'][:, 0:512], 'wk', 'Wq', 8, 128, 512)
        wload(Wqs, W['w_sw'][:, 0:512], 'wk', 'Wq', 8, 128, 512)
        P.dma('sp', 'stg', lambda e: e.dma_start(out=stg[:, 0:5120], in_=W['bt'][:, 0:5120]), writes=['stg'])
        P.op('act', lambda e: e.activation(out=Eg[:], in_=stg[:, 0:5120], func=AF.Exp), reads=['stg'], writes=['E'])
        for j in range(4):
            P.dma('sp', 'stg', lambda e, j=j: e.dma_start(out=stg[:], in_=W['bt'][:, 5120 + j * 6144: 5120 + (j + 1) * 6144]),
                  writes=['stg'])
            P.op('act', lambda e, j=j: e.activation(out=Es[j][:], in_=stg[:], func=AF.Exp), reads=['stg'], writes=['E'])
        P.dma('sp', 'kc', lambda e: e.dma_start(out=sview(Kc, 256, 4),
                                                in_=KS[:, TX:TX + 256].rearrange("(c p) n -> p c n", p=128)), writes=['Kc'])
        P.dma('sp', 'kc', lambda e: e.dma_start(out=Vc[:].rearrange("p (s e) -> p s e", e=520),
                                                in_=VS[NS:NS + 2].rearrange("s p e -> p s e")), writes=['Kc'])
        bQA, rQA = PP.fixed(6)
        bQB, rQB = PP.fixed(7)
        Sb = [PP.fixed(0, 2), PP.fixed(2, 2)]
        Ob, rO = PP.fixed(4, 2)
        blkc = [0]
        pending = [None]

        def do_tile(T):
            N, s, t0 = T['N'], T['s'], T['t0']
            P.dma('sp', 'hxl', lambda e, t0=t0, N=N: e.dma_start(out=sview(hx, N), in_=xview(HX, t0, N)), writes=['hx'])
            if s == 0:
                P.dma('sp', 'cs', lambda e, t0=t0: e.dma_start(out=cs_t[:], in_=cosT[:, t0:t0 + 512]), writes=['cs'])
                P.dma('sp', 'cs', lambda e, t0=t0: e.dma_start(out=sn_t[:], in_=sinT[:, t0:t0 + 512]), writes=['cs'])
            for kc in range(4):
                def mq(e, kc=kc, bank=bQA, Wm=Wq):
                    ins = None
                    for k in range(8):
                        ins = e.matmul(bank[:, 0:N], Wm[:, k * 512 + kc * 128: k * 512 + (kc + 1) * 128],
                                       hx[:, k * N:(k + 1) * N], start=(k == 0), stop=(k == 7))
                    return ins
                P.op('pe', mq, reads=['hx', 'Wq'], writes=rQA)
                if s == 0:
                    P.op('pe', lambda e, kc=kc: mq(e, kc, bQB, Wqs), reads=['hx', 'Wq'], writes=rQB)
                    P.op('dve', lambda e: e.tensor_tensor(out=t1[:, 0:N], in0=bQA[:, 0:N], in1=cs_t[:, 0:N], op=ALU.mult),
                         reads=rQA + ['cs'], writes=['t1'])
                    P.op('dve', lambda e: e.tensor_tensor(out=t2[:, 0:N], in0=bQB[:, 0:N], in1=sn_t[:, 0:N], op=ALU.mult),
                         reads=rQB + ['cs'], writes=['t2'])
                    P.op('pool', lambda e, kc=kc: e.tensor_tensor(out=Q[:, kc * N:(kc + 1) * N], in0=t1[:, 0:N], in1=t2[:, 0:N], op=ALU.add),
                         reads=['t1', 't2'], writes=['Q'])
                else:
                    P.op('act', lambda e, kc=kc: e.activation(out=Q[:, kc * N:(kc + 1) * N], in_=bQA[:, 0:N], func=AF.Copy),
                         reads=rQA, writes=['Q'])
            for qb in range(N // 128):
                bb = blkc[0] % 2
                blkc[0] += 1
                if s == 0:
                    slot = t0 // 128 + qb
                    p = slot - HALO
                    lo_o, hi_o = -2, 2
                    etab, NO, eo0 = Eg, 5, 0
                    if p == 0:
                        hi_o = 3
                        etab, NO = Es[0], 6
                    elif p == 1:
                        etab, NO = Es[1], 6
                    elif p == NP_OWN - 2:
                        etab, NO = Es[2], 6
                    elif p == NP_OWN - 1:
                        lo_o = -3
                        etab, NO = Es[3], 6
                    lo = max(slot + lo_o, 0)
                    hi = min(slot + hi_o, NS - 1)
                    eo0 = lo - (slot + lo_o)
                    nk = hi - lo + 1
                    P.dma('sp', f'kb{bb}', lambda e, lo=lo, nk=nk, bb=bb: e.dma_start(
                        out=Kb[bb][:].rearrange("p (c n) -> p c n", c=4)[:, :, 0:nk * 128],
                        in_=KS[:, lo * 128:(lo + nk) * 128].rearrange("(c p) n -> p c n", p=128)),
                        writes=[f'Kb{bb}'])
                    P.dma('sp', f'kb{bb}', lambda e, lo=lo, nk=nk, bb=bb: e.dma_start(
                        out=Vb[bb][:, 0:nk * 520].rearrange("p (s e) -> p s e", e=520),
                        in_=VS[lo:lo + nk].rearrange("s p e -> p s e")),
                        writes=[f'Kb{bb}'])
                else:
                    nk = 0
                ntot = nk + 2
                pvs = []
                for h in range(NH):
                    hc, pb = h // 2, 64 * (h % 2)
                    Sps, rS = Sb[h % 2]
                    Pbuf = Pb[h % 2]

                    def qk(e, hc=hc, pb=pb, Sps=Sps, nk=nk, bb=bb, qb=qb):
                        ins = None
                        qv = Q[pb:pb + 64, hc * N + qb * 128: hc * N + (qb + 1) * 128]
                        for kt_ in range(nk):
                            ins = e.matmul(Sps[:, kt_ * 128:(kt_ + 1) * 128],
                                           Kb[bb][pb:pb + 64, hc * 768 + kt_ * 128: hc * 768 + (kt_ + 1) * 128],
                                           qv, start=True, stop=True)
                        for ct in range(2):
                            ins = e.matmul(Sps[:, (nk + ct) * 128:(nk + ct + 1) * 128],
                                           Kc[pb:pb + 64, hc * 256 + ct * 128: hc * 256 + (ct + 1) * 128],
                                           qv, start=True, stop=True)
                        return ins
                    P.op('pe', qk, reads=['Q', f'Kb{bb}', 'Kc'], writes=rS)
                    P.op('act', lambda e, Sps=Sps, Pbuf=Pbuf, ntot=ntot: e.activation(
                        out=Pbuf[:, 0:ntot * 128], in_=Sps[:, 0:ntot * 128], func=AF.Exp, scale=HD ** -0.5),
                        reads=rS, writes=[f'P{h % 2}'])
                    if nk > 0:
                        e0 = (h * NO + eo0) * 128
                        P.op('pool', lambda e, Pbuf=Pbuf, etab=etab, e0=e0, nk=nk: e.tensor_tensor(
                            out=Pbuf[:, 0:nk * 128], in0=Pbuf[:, 0:nk * 128], in1=etab[:, e0:e0 + nk * 128], op=ALU.mult),
                            reads=[f'P{h % 2}', 'E'], writes=[f'P{h % 2}'])

                    def pv(e, h=h, Pbuf=Pbuf, nk=nk, bb=bb):
                        ins = None
                        for kt_ in range(nk):
                            ins = e.matmul(Ob[:, h * 128:h * 128 + 65], Pbuf[:, kt_ * 128:(kt_ + 1) * 128],
                                           Vb[bb][:, kt_ * 520 + h * 65: kt_ * 520 + h * 65 + 65],
                                           start=(kt_ == 0), stop=False)
                        for ct in range(2):
                            ins = e.matmul(Ob[:, h * 128:h * 128 + 65], Pbuf[:, (nk + ct) * 128:(nk + ct + 1) * 128],
                                           Vc[:, ct * 520 + h * 65: ct * 520 + h * 65 + 65],
                                           start=(nk == 0 and ct == 0), stop=(ct == 1))
                        return ins

                    def emit_pv(h=h, pv=pv, bb=bb):
                        P.op('pe', pv, reads=[f'P{h % 2}', f'Kb{bb}', 'Kc'], writes=[f'O{h}'] + (rO if h == 0 else []))
                    pvs.append(emit_pv)
                    if h >= 1:
                        pvs[h - 1]()
                    if h == 1 and pending[0] is not None:
                        pending[0]()
                        pending[0] = None
                pvs[NH - 1]()
                on = [f'O{h}' for h in range(NH)]
                P.op('dve', lambda e: e.reciprocal(
                    out=rden[:], in_=Ob[:].rearrange("p (h e) -> p h e", e=128)[:, :, 64]),
                    reads=on, writes=['rden'])
                for h in range(NH):
                    P.op('dve', lambda e, h=h: e.tensor_scalar(
                        out=otok[:, h * 64:(h + 1) * 64], in0=Ob[:, h * 128:h * 128 + 64],
                        scalar1=rden[:, h:h + 1], scalar2=None, op0=ALU.mult),
                        reads=on + ['rden'], writes=['otok'])
                tbank, rtb = (bQA, rQA) if blkc[0] % 2 == 0 else (bQB, rQB)

                def tail2(qb=qb, tbank=tbank, rtb=rtb):
                    def tp(e):
                        ins = None
                        for hc in range(4):
                            ins = e.matmul(tbank[:, hc * 128:(hc + 1) * 128], otok[:, hc * 128:(hc + 1) * 128],
                                           ident_bf[:], start=True, stop=True)
                        return ins
                    P.op('pe', tp, reads=['otok', 'ident'], writes=rtb)
                    P.op('act', lambda e: e.activation(
                        out=sview(oaT, N, 4)[:, :, qb * 128:(qb + 1) * 128],
                        in_=tbank[:, 0:512].rearrange("p (c n) -> p c n", c=4), func=AF.Copy),
                        reads=rtb, writes=['oaT'])
                pending[0] = tail2
            if pending[0] is not None:
                pending[0]()
                pending[0] = None
            P.dma('sp', 'oas', lambda e, t0=t0, N=N: e.dma_start(
                out=OA[:, t0:t0 + N].rearrange("(c p) n -> p c n", p=128), in_=sview(oaT, N, 4)),
                reads=['oaT'], writes=[f'OA:{T["idx"]}'])
        for T in tiles:
            do_tile(T)
        P.barrier()

    def mix_sweep(l, tiles, Xsrc, Xdst):
        W = Wl[l]
        A.reset(base_mark)
        PP.set_order([4, 5, 6, 7])
        Wu = A.alloc([128, 8 * 512], BF)
        Wvs = A.alloc([128, 8 * 512], BF)
        Wg = A.alloc([128, 8 * 2048], BF)
        Wpa = A.alloc([128, 4 * D], BF)
        Wpb = A.alloc([128, 4 * D], BF)
        Wo = A.alloc([128, 8 * D], BF)
        wsT = A.alloc([128, 512], BF)
        lnG = A.alloc([128, 512], F32)
        lnB = A.alloc([128, 512], F32)
        bsf = A.alloc([64, 512], F32)
        bsf2 = A.alloc([64, 512], F32)
        bsz = A.alloc([64, 512], BF)
        bg = A.alloc([128, 16], F32)
        hxs_ = [A.alloc([128, 8 * 512], BF) for _ in range(2)]
        xts_ = [A.alloc([128, 8 * 512], F32) for _ in range(2)]
        oas_ = [A.alloc([128, 4 * 512], BF) for _ in range(2)]
        u = A.alloc([128, 4 * 512], BF)
        ob = A.alloc([128, 4 * 512], BF)
        gates = A.alloc([128, 16 * 512], BF)
        m = A.alloc([128, 8 * 512], BF)
        vg = [A.alloc([128, 512], F32) for _ in range(4)]
        vnb = [A.alloc([128, 512], BF) for _ in range(4)]
        st6 = A.alloc([128, 6], F32)
        mv2 = A.alloc([128, 2], F32)
        sd = A.alloc([128, 1], F32)
        t1 = A.alloc([128, 512], F32)
        t2 = A.alloc([128, 512], F32)
        wload(Wu, W['w_in'][:, 1536:2048], 'wk', 'Wm', 8, 128, 512)
        wload(Wvs, W['w_in'][:, 2048:2560], 'wk', 'Wm', 8, 128, 512)
        wload(Wg, W['w_in'][:, 2560:4608], 'wk', 'Wm', 8, 128, 2048)
        wload(Wo, W['w_o'], 'wk', 'Wm', 8, 128, D)
        wload(Wpa, W['w_pa'], 'wk', 'Wm', 4, 128, D)
        wload(Wpb, W['w_pb'], 'wk', 'Wm', 4, 128, D)
        P.dma('pool', 'wk', lambda e: e.dma_start(out=wsT[:], in_=W['wsT']), writes=['Wm'])
        P.dma('sp', 'c1', lambda e: e.dma_start(out=lnG[:], in_=W['lng']), writes=['ln'])
        P.dma('sp', 'c1', lambda e: e.dma_start(out=lnB[:], in_=W['lnb']), writes=['ln'])
        P.dma('sp', 'c3', lambda e: e.dma_start(out=bg[:], in_=W['bgT']), writes=['bg'])
        P.op('dve', lambda e: e.memset(bsf[:], 0.0), writes=['bsf'])
        P.op('dve', lambda e: e.memset(bsz[:], 0.0), writes=['bsz'])
        P.dma('sp', 'c2', lambda e: e.dma_start(out=bsf[0:1, :], in_=W['bs']), reads=['bsf'], writes=['bsf'])
        P.dma('sp', 'c2', lambda e: e.dma_start(out=bsf[32:33, :], in_=W['bs']), reads=['bsf'], writes=['bsf'])
        P.op('dve', lambda e: e.tensor_copy(out=bsz[0:1, :], in_=bsf[0:1, :]), reads=['bsf', 'bsz'], writes=['bsz'])
        P.op('dve', lambda e: e.tensor_copy(out=bsf2[32:33, :], in_=bsz[0:1, :]), reads=['bsz'], writes=['bsf2'])
        P.op('dve', lambda e: e.tensor_tensor(out=bsz[32:33, :], in0=bsf[32:33, :], in1=bsf2[32:33, :], op=ALU.subtract),
             reads=['bsf', 'bsf2', 'bsz'], writes=['bsz'])

        def load(ti, b):
            T = tiles[ti]
            t0, N = T['t0'], T['N']
            P.dma('sp', f'hxl{b}', lambda e: e.dma_start(out=sview(hxs_[b], N), in_=xview(HX, t0, N)), writes=[f'hx{b}'])
            P.dma('sp', f'x{b}', lambda e: e.dma_start(out=sview(xts_[b], N), in_=xview(Xsrc, t0, N)), writes=[f'xt{b}'])
            P.dma('sp', f'oal{b}', lambda e: e.dma_start(
                out=sview(oas_[b], N, 4), in_=OA[:, t0:t0 + N].rearrange("(c p) n -> p c n", p=128)), writes=[f'oaT{b}'])
        load(0, 0)

        def do_tile(ti, T):
            b = ti % 2
            N, s, t0 = T['N'], T['s'], T['t0']
            nsub = N // 128
            if ti + 1 < len(tiles):
                load(ti + 1, 1 - b)
            hx, xt, oaT = hxs_[b], xts_[b], oas_[b]
            rhx, rxt, roa = f'hx{b}', f'xt{b}', f'oaT{b}'
            for kc in range(4):
                bU, rU = PP.bank()

                def mu(e, kc=kc, bU=bU):
                    ins = None
                    for k in range(8):
                        ins = e.matmul(bU[:, 0:N], Wu[:, k * 512 + kc * 128: k * 512 + (kc + 1) * 128],
                                       hx[:, k * N:(k + 1) * N], start=(k == 0), stop=(k == 7))
                    return ins
                P.op('pe', mu, reads=[rhx, 'Wm'], writes=[rU])
                P.op('act', lambda e, kc=kc, bU=bU: e.activation(out=u[:, kc * N:(kc + 1) * N], in_=bU[:, 0:N],
                                                                func=AF.Gelu_apprx_tanh),
                     reads=[rU], writes=['u'])
            for sub in range(nsub):
                bV, rV = PP.bank()
                vgb, vnbb = vg[sub], vnb[sub]
                rvg, rvn = f'vg{sub}', f'vnb{sub}'

                def mvs(e, sub=sub, bV=bV):
                    ins = None
                    for k in range(8):
                        ins = e.matmul(bV[:, 0:512], hx[:, k * N + sub * 128: k * N + (sub + 1) * 128],
                                       Wvs[:, k * 512:(k + 1) * 512], start=(k == 0), stop=(k == 7))
                    return ins
                P.op('pe', mvs, reads=[rhx, 'Wm'], writes=[rV])
                P.op('act', lambda e, bV=bV, vgb=vgb: e.activation(out=vgb[:], in_=bV[:, 0:512], func=AF.Gelu_apprx_tanh),
                     reads=[rV], writes=[rvg])
                P.op('dve', lambda e, vgb=vgb: e.bn_stats(out=st6[:], in_=vgb[:]), reads=[rvg], writes=['st6'])
                P.op('dve', lambda e: e.bn_aggr(out=mv2[:], in_=st6[:]), reads=['st6'], writes=['mv2'])
                P.op('act', lambda e: e.activation(out=sd[:], in_=mv2[:, 1:2], func=AF.Sqrt, bias=eps_t[:], scale=1.0),
                     reads=['mv2', 'eps'], writes=['sd'])
                P.op('dve', lambda e: e.reciprocal(out=sd[:], in_=sd[:]), reads=['sd'], writes=['sd'])
                P.op('dve', lambda e, vgb=vgb: e.tensor_scalar(out=vgb[:], in0=vgb[:], scalar1=mv2[:, 0:1], scalar2=sd[:, 0:1],
                                                               op0=ALU.subtract, op1=ALU.mult),
                     reads=[rvg, 'mv2', 'sd'], writes=[rvg])
                P.op('dve', lambda e, vgb=vgb: e.tensor_tensor(out=vgb[:], in0=vgb[:], in1=lnG[:], op=ALU.mult),
                     reads=[rvg, 'ln'], writes=[rvg])
                P.op('pool', lambda e, vgb=vgb, vnbb=vnbb: e.tensor_tensor(out=vnbb[:], in0=vgb[:], in1=lnB[:], op=ALU.add),
                     reads=[rvg, 'ln'], writes=[rvn])
            for gc in range(16):
                bG, rG = PP.bank()

                def mg(e, gc=gc, bG=bG):
                    ins = None
                    for k in range(8):
                        ins = e.matmul(bG[:, 0:N], Wg[:, k * 2048 + gc * 128: k * 2048 + (gc + 1) * 128],
                                       hx[:, k * N:(k + 1) * N], start=(k == 0), stop=(k == 7))
                    return ins
                P.op('pe', mg, reads=[rhx, 'Wm'], writes=[rG])
                P.op('act', lambda e, gc=gc, bG=bG: e.activation(out=gates[:, gc * N:(gc + 1) * N], in_=bG[:, 0:N],
                                                                func=AF.Sigmoid, bias=bg[:, gc:gc + 1], scale=1.0),
                     reads=[rG, 'bg'], writes=['gates'])
            for sub in range(nsub):
                vnbb, rvn = vnb[sub], f'vnb{sub}'

                def msg(e, sub=sub, vnbb=vnbb):
                    ins = None
                    for g in range(4):
                        o = ps[:, g * 512 + sub * 128: g * 512 + (sub + 1) * 128]
                        e.matmul(o, vnbb[:, g * 128:(g + 1) * 128], wsT[:, g * 128:(g + 1) * 128], start=True, stop=False)
                        ins = e.matmul(o, onesz[0:33, :], bsz[0:33, g * 128:(g + 1) * 128], start=False, stop=True)
                    return ins
                P.op('pe', msg, reads=[rvn, 'Wm', 'bsz', 'onesz'], writes=['psg'])
            for g in range(4):
                P.op('dve', lambda e, g=g: e.tensor_tensor(out=ob[:, g * N:(g + 1) * N], in0=ps[:, g * 512:g * 512 + N],
                                                           in1=u[:, g * N:(g + 1) * N], op=ALU.mult),
                     reads=['psg', 'u'], writes=['ob'])
            for oc in range(8):
                bA, rA = PP.bank()
                bB, rB = PP.bank()

                def mp(e, oc=oc, bank=bA, Wm=Wpa, src=oaT):
                    ins = None
                    for k in range(4):
                        ins = e.matmul(bank[:, 0:N], Wm[:, k * D + oc * 128: k * D + (oc + 1) * 128],
                                       src[:, k * N:(k + 1) * N], start=(k == 0), stop=(k == 3))
                    return ins
                P.op('pe', mp, reads=[roa, 'Wm'], writes=[rA])
                P.op('pe', lambda e, oc=oc, bB=bB: mp(e, oc, bB, Wpb, ob), reads=['ob', 'Wm'], writes=[rB])
                P.op('dve', lambda e, oc=oc, bA=bA: e.tensor_tensor(out=t1[:, 0:N], in0=bA[:, 0:N],
                                                                   in1=gates[:, oc * N:(oc + 1) * N], op=ALU.mult),
                     reads=[rA, 'gates'], writes=['t1'])
                P.op('dve', lambda e, oc=oc, bB=bB: e.tensor_tensor(out=t2[:, 0:N], in0=bB[:, 0:N],
                                                                   in1=gates[:, (8 + oc) * N:(9 + oc) * N], op=ALU.mult),
                     reads=[rB, 'gates'], writes=['t2'])
                P.op('pool', lambda e, oc=oc: e.tensor_tensor(out=m[:, oc * N:(oc + 1) * N], in0=t1[:, 0:N], in1=t2[:, 0:N], op=ALU.add),
                     reads=['t1', 't2'], writes=['m'])
            for oc in range(8):
                bY, rY = PP.bank()

                def mo(e, oc=oc, bY=bY):
                    ins = None
                    for k in range(8):
                        ins = e.matmul(bY[:, 0:N], Wo[:, k * D + oc * 128: k * D + (oc + 1) * 128],
                                       m[:, k * N:(k + 1) * N], start=(k == 0), stop=(k == 7))
                    return ins
                P.op('pe', mo, reads=['m', 'Wm'], writes=[rY])
                P.op('dve', lambda e, oc=oc, bY=bY: e.scalar_tensor_tensor(
                    out=xt[:, oc * N:(oc + 1) * N], in0=bY[:, 0:N], scalar=mvcol(5, oc, s),
                    in1=xt[:, oc * N:(oc + 1) * N], op0=ALU.mult, op1=ALU.add),
                    reads=[rY, rxt, 'mv'], writes=[rxt])
            P.dma('sp', f'xs{b}', lambda e: e.dma_start(out=xview(Xdst, t0, N), in_=sview(xt, N)),
                  reads=[rxt], writes=[f'XD:{T["idx"]}'])
        for ti, T in enumerate(tiles):
            do_tile(ti, T)
        P.barrier()

    allx = xtiles(0, NT)
    own = xtiles(HALO // 4, NT - HALO // 4)
    mid = ([dict(t0=2 * 128, N=256, s=0, idx=100)] + own +
           [dict(t0=(NS - 4) * 128, N=256, s=0, idx=101)])
    for l in range(DEPTH):
        last = (l == DEPTH - 1)
        Xin = xT_in if l == 0 else XS[2]
        first = allx if l == 0 else mid
        layer_setup(l)
        ffn_sweep(l, 0, first + [CT], Xin, XS[0])
        kv_sweep(l, first + [CT], XS[0])
        if not last:
            att_sweep(l, mid + [CT])
            mix_sweep(l, mid + [CT], XS[0], XS[1])
            ffn_sweep(l, 2, mid + [CT], XS[1], XS[2])
        else:
            att_sweep(l, own)
            mix_sweep(l, own, XS[0], XS[1])
            ffn_sweep(l, 2, own, XS[1], None, final=True)

    with nc.Block() as block:
        P.emit(block)
    return nc


def _fm(v):
    v = np.asarray(v, np.float32)
    return np.ascontiguousarray(v.reshape(-1, 128).T)


def _bias_tables(rpb_l, j, NPG):
    rows = 2 * NPG
    kk = np.arange(128)
    qq = np.arange(128)

    def table(gq, offs, npad):
        out = np.full((128, NH, npad, 128), NEG, np.float32)
        qr = 2 * gq + qq // 64
        qc = qq % 64
        rs = np.clip(qr - 4, 0, rows - 8)
        cs = np.clip(qc - 8, 0, GW - 16)
        for oi, o in enumerate(offs):
            gk = gq + o
            if gk < 0 or gk >= NPG:
                continue
            kr = 2 * gk + kk // 64
            kc = kk % 64
            valid = ((kr[:, None] >= rs[None, :]) & (kr[:, None] < rs[None, :] + 8) &
                     (kc[:, None] >= cs[None, :]) & (kc[:, None] < cs[None, :] + 16))
            dr = np.clip(kr[:, None] - qr[None, :] + 7, 0, 14)
            dc = np.clip(kc[:, None] - qc[None, :] + 15, 0, 30)
            g = rpb_l[:, dr, dc]
            g = np.where(valid[None], g, np.float32(NEG))
            out[:, :, oi, :] = np.transpose(g, (1, 0, 2))
        return out.reshape(128, -1)

    gen = table(NPG // 2, [-2, -1, 0, 1, 2], 5)
    g0 = j * NP_OWN
    sp = [table(g0 + 0, [-2, -1, 0, 1, 2, 3], 6),
          table(g0 + 1, [-2, -1, 0, 1, 2], 6),
          table(g0 + NP_OWN - 2, [-2, -1, 0, 1, 2], 6),
          table(g0 + NP_OWN - 1, [-3, -2, -1, 0, 1, 2], 6)]
    return np.ascontiguousarray(np.concatenate([gen] + sp, axis=1))


def _rope_tables(j, NPG, NS):
    d = np.arange(128) % 64
    i = d // 2
    par = d % 2
    n_freq = HD // 4
    freqs = (np.float32(10000.0) ** (-np.arange(n_freq, dtype=np.float32) / np.float32(n_freq))).astype(np.float32)
    t = np.arange(NS * 128)
    slot = t // 128
    g = j * NP_OWN - HALO + slot
    row = (2 * g + (t % 128) // 64).astype(np.float32)
    col = (t % 64).astype(np.float32)
    fr = freqs[i % n_freq]
    ang = np.where((i < n_freq)[:, None], row[None, :] * fr[:, None], col[None, :] * fr[:, None]).astype(np.float32)
    cos = np.cos(ang).astype(np.float32)
    sin = np.sin(ang).astype(np.float32)
    sins = np.where((par == 0)[:, None], -sin, sin).astype(np.float32)
    return np.ascontiguousarray(cos), np.ascontiguousarray(sins)


_NC_CACHE = {}


def kernel(x, c, ctx, c_ctx, w_ada, b_ada, norm_g, w_ff1_up, w_ff1_down, w_in, b_gate,
           rpb, ln_v_g, ln_v_b, w_s, b_s, w_pa, w_pb, w_o, w_ff2_up, w_ff2_down, final_g):
    C = cfg()
    NS, TX, TALL, NPG = C['NS'], C['TX'], C['TALL'], C['NPG']
    f = lambda a: np.ascontiguousarray(np.asarray(a, np.float32))
    x, c, ctx, c_ctx = f(x), f(c), f(ctx), f(c_ctx)
    B = x.shape[0]
    assert x.shape[1] == NPG * 128 and B * CPB == NCORE
    swap = np.arange(512) ^ 1
    shared = {"ident": np.eye(128, dtype=np.float32), "fgT": _fm(final_g)}
    for l in range(DEPTH):
        wi = f(w_in[l])
        shared[f"w_ada{l}"] = f(w_ada[l])
        shared[f"b_adaT{l}"] = _fm(b_ada[l])
        shared[f"ngT{l}"] = _fm(np.asarray(norm_g[l]).reshape(-1))
        shared[f"up1_{l}"] = f(w_ff1_up[l]); shared[f"dn1_{l}"] = f(w_ff1_down[l])
        shared[f"up2_{l}"] = f(w_ff2_up[l]); shared[f"dn2_{l}"] = f(w_ff2_down[l])
        shared[f"w_in{l}"] = wi
        shared[f"w_sw{l}"] = np.ascontiguousarray(np.concatenate([wi[:, 0:512][:, swap], wi[:, 512:1024][:, swap]], axis=1))
        shared[f"bgT{l}"] = _fm(b_gate[l])
        shared[f"lng{l}"] = np.ascontiguousarray(np.broadcast_to(f(ln_v_g[l])[None, :], (128, 512)))
        shared[f"lnb{l}"] = np.ascontiguousarray(np.broadcast_to(f(ln_v_b[l])[None, :], (128, 512)))
        shared[f"wsT{l}"] = np.ascontiguousarray(np.transpose(f(w_s[l]), (2, 0, 1)).reshape(128, 512))
        shared[f"bs{l}"] = f(b_s[l]).reshape(1, 512)
        shared[f"w_pa{l}"] = f(w_pa[l]); shared[f"w_pb{l}"] = f(w_pb[l]); shared[f"w_o{l}"] = f(w_o[l])
    rpb = f(rpb)
    in_maps = []
    for core in range(NCORE):
        b, j = core // CPB, core % CPB
        xT = np.zeros((D, TALL), np.float32)
        g_lo = j * NP_OWN - HALO
        v_lo, v_hi = max(g_lo, 0), min(g_lo + NS, NPG)
        xT[:, (v_lo - g_lo) * 128:(v_hi - g_lo) * 128] = x[b, v_lo * 128:v_hi * 128, :].T
        xT[:, TX:] = ctx[b].T
        cosT, sinT = _rope_tables(j, NPG, NS)
        cond = np.stack([_fm(c[b]), _fm(c_ctx)], axis=-1).reshape(128, 16)
        m = dict(shared)
        m.update({"xT": xT, "cosT": cosT, "sinT": sinT, "condT": np.ascontiguousarray(cond)})
        for l in range(DEPTH):
            m[f"bt{l}"] = _bias_tables(rpb[l], j, NPG)
        in_maps.append(m)
    key = (NP_OWN, DEBUG_OUT)
    if key not in _NC_CACHE:
        _NC_CACHE[key] = build_program()
    nc = _NC_CACHE[key]
    res = run_bass_kernel_spmd(nc, in_maps, core_ids=list(range(NCORE)))
    out = np.zeros((B, NPG * 128, D), np.float32)
    for core in range(NCORE):
        b, j = core // CPB, core % CPB
        out[b, j * NP_OWN * 128:(j + 1) * NP_OWN * 128, :] = np.asarray(res.results[core]["yT"], np.float32).T
    if DEBUG_OUT:
        kernel.last_results = res.results
    return out
```
